# Optimizing a Trainium2 kernel written in Bass

```python
import math
import jax, jax.numpy as jnp
from jax import lax
import numpy as np

D_MODEL = 2048
BATCH = 8
SEQ = 2048
DEPTH = 1

HEAD_DIM = 128
N_HEADS_A = 8
D_A = N_HEADS_A * HEAD_DIM
DILATION_PATTERNS = ((128, 1), (512, 4), (2048, 16))
N_BUCKETS = 32
MAX_DISTANCE = 2048
CHUNK = 128
N_GROUPS_B = 8
D_GROUP_B = 128
D_B = N_GROUPS_B * D_GROUP_B
D_FF = 4 * D_MODEL
D_IN = 3 * D_A + 2 * D_B + 2 * D_MODEL
SPLITS = tuple(np.cumsum([D_A, D_A, D_A, D_B, D_B, D_MODEL]).tolist())
ALPHA = (2 * DEPTH) ** 0.25
BETA = (8 * DEPTH) ** -0.25
LN_EPS = 1e-5
NEG_INF = -1e30

kernel_name = "hybrid_dilated_attn_gmlp_block"


def layer_norm(x, gain, bias):
    xf = x.astype(jnp.float32)
    mean = jnp.mean(xf, axis=-1, keepdims=True)
    var = jnp.mean(jnp.square(xf - mean), axis=-1, keepdims=True)
    y = (xf - mean) * lax.rsqrt(var + LN_EPS) * gain.astype(jnp.float32) + bias.astype(jnp.float32)
    return y.astype(x.dtype)


def t5_causal_bucket(n):
    max_exact = N_BUCKETS // 2
    nf = jnp.maximum(n, 1).astype(jnp.float32)
    large = max_exact + (jnp.log(nf / max_exact) / math.log(MAX_DISTANCE / max_exact)
                         * (N_BUCKETS - max_exact)).astype(jnp.int32)
    large = jnp.minimum(large, N_BUCKETS - 1)
    return jnp.where(n < max_exact, n, large)


def dilated_window_attention(q, k, v, rel_bias, window, dilation):
    B, S, H, Dh = q.shape
    nb = window // dilation
    L = S // dilation
    nblk = -(-L // nb)
    Lp = nblk * nb

    def to_sub(t):
        t = t.reshape(B, L, dilation, H, Dh).transpose(0, 2, 1, 3, 4)
        t = jnp.pad(t, ((0, 0), (0, 0), (0, Lp - L), (0, 0), (0, 0)))
        return t.reshape(B, dilation, nblk, nb, H, Dh)

    def with_prev(t):
        prev = jnp.concatenate([jnp.zeros_like(t[:, :, :1]), t[:, :, :-1]], axis=2)
        return jnp.concatenate([prev, t], axis=3)

    qs = to_sub(q)
    kw = with_prev(to_sub(k))
    vw = with_prev(to_sub(v))

    scores = jnp.einsum('brnqhd,brnkhd->brnhqk', qs, kw).astype(jnp.float32) * (HEAD_DIM ** -0.5)

    qi = jnp.arange(nb)[:, None]
    kj = jnp.arange(2 * nb)[None, :]
    steps = nb + qi - kj
    band = (steps >= 0) & (steps <= nb)
    blk = jnp.arange(nblk)[:, None, None]
    key_ok = (blk * nb + kj[None] - nb) >= 0
    mask = band[None] & key_ok
    bucket = t5_causal_bucket(jnp.clip(steps, 0, nb) * dilation)
    bias = jnp.transpose(rel_bias[bucket].astype(jnp.float32), (2, 0, 1))

    scores = jnp.where(mask[None, None, :, None], scores + bias[None, None, None], NEG_INF)
    m = jnp.max(scores, axis=-1, keepdims=True)
    p = jnp.exp(scores - m)
    den = jnp.sum(p, axis=-1, keepdims=True)
    out = jnp.einsum('brnhqk,brnkhd->brnqhd', p, vw.astype(jnp.float32))
    den_t = jnp.transpose(den[..., 0], (0, 1, 2, 4, 3))
    out = out / den_t[..., None]
    lse = jnp.transpose(m[..., 0], (0, 1, 2, 4, 3)) + jnp.log(den_t)

    def from_sub(t):
        rest = t.shape[5:]
        t = t.reshape((B, dilation, Lp, H) + rest)[:, :, :L]
        t = jnp.moveaxis(t, 1, 2)
        return t.reshape((B, S, H) + rest)

    return from_sub(out), from_sub(lse)


def token_mixers(x, w_in, rel_bias, ln_v_gain, ln_v_bias, w_spatial, b_spatial,
                 w_proj_a, w_proj_b, w_out):
    B, S, _ = x.shape
    proj = jnp.einsum('bsd,de->bse', x, w_in)
    q, k, v, u, vb, ga, gb = jnp.split(proj, SPLITS, axis=-1)

    q = q.reshape(B, S, N_HEADS_A, HEAD_DIM)
    k = k.reshape(B, S, N_HEADS_A, HEAD_DIM)
    v = v.reshape(B, S, N_HEADS_A, HEAD_DIM)
    outs, lses = [], []
    for window, dilation in DILATION_PATTERNS:
        o, l = dilated_window_attention(q, k, v, rel_bias, window, dilation)
        outs.append(o)
        lses.append(l)
    mix_w = jax.nn.softmax(jnp.stack(lses, axis=0), axis=0)
    attn = jnp.sum(mix_w[..., None] * jnp.stack(outs, axis=0), axis=0)
    attn = attn.astype(x.dtype).reshape(B, S, D_A)

    u = jax.nn.gelu(u)
    vb = layer_norm(jax.nn.gelu(vb), ln_v_gain, ln_v_bias)
    nc = S // CHUNK
    vr = vb.reshape(B, nc, CHUNK, N_GROUPS_B, D_GROUP_B)
    causal = jnp.tril(jnp.ones((CHUNK, CHUNK), dtype=bool))
    ws = jnp.where(causal[None], w_spatial, 0.0).astype(vr.dtype)
    z = jnp.einsum('gij,bcjgd->bcigd', ws, vr) + jnp.transpose(b_spatial)[None, None, :, :, None]
    gmlp = (u.reshape(B, nc, CHUNK, N_GROUPS_B, D_GROUP_B) * z).reshape(B, S, D_B)

    y_a = jnp.einsum('bse,ed->bsd', attn, w_proj_a)
    y_b = jnp.einsum('bse,ed->bsd', gmlp, w_proj_b)
    merged = jax.nn.sigmoid(ga) * y_a + jax.nn.sigmoid(gb) * y_b
    return jnp.einsum('bsd,de->bse', merged, w_out)


def squared_relu_mlp(h, w_ff1, b_ff1, w_ff2, b_ff2):
    a = jnp.square(jax.nn.relu(jnp.einsum('bsd,df->bsf', h, w_ff1) + b_ff1))
    return jnp.einsum('bsf,fd->bsd', a, w_ff2) + b_ff2


def setup_inputs(seed: int = 0) -> dict:
    key = jax.random.key(seed)
    ks = jax.random.split(key, 20)
    nrm = jax.random.normal
    f32 = jnp.float32
    x = nrm(ks[0], (BATCH, SEQ, D_MODEL), f32)
    col_scale = jnp.concatenate([
        jnp.ones((2 * D_A,), f32),
        BETA * jnp.ones((D_A,), f32),
        jnp.ones((2 * D_B + 2 * D_MODEL,), f32)]) * (D_MODEL ** -0.5)
    w_in = nrm(ks[1], (DEPTH, D_MODEL, D_IN), f32) * col_scale
    rel_bias = 0.1 * nrm(ks[2], (N_BUCKETS, N_HEADS_A), f32)
    ln_v_gain = 1.0 + 0.01 * nrm(ks[3], (DEPTH, D_B), f32)
    ln_v_bias = 0.01 * nrm(ks[4], (DEPTH, D_B), f32)
    w_spatial = nrm(ks[5], (DEPTH, N_GROUPS_B, CHUNK, CHUNK), f32) * (CHUNK ** -0.5)
    b_spatial = 1.0 + 0.01 * nrm(ks[6], (DEPTH, N_GROUPS_B, CHUNK), f32)
    w_proj_a = nrm(ks[7], (DEPTH, D_A, D_MODEL), f32) * (D_A ** -0.5) * BETA
    w_proj_b = nrm(ks[8], (DEPTH, D_B, D_MODEL), f32) * (D_B ** -0.5) * BETA
    w_out = nrm(ks[9], (DEPTH, D_MODEL, D_MODEL), f32) * (D_MODEL ** -0.5) * BETA
    ln1_gain = 1.0 + 0.01 * nrm(ks[10], (DEPTH, D_MODEL), f32)
    ln1_bias = 0.01 * nrm(ks[11], (DEPTH, D_MODEL), f32)
    w_ff1 = nrm(ks[12], (DEPTH, D_MODEL, D_FF), f32) * (D_MODEL ** -0.5) * BETA
    b_ff1 = 0.01 * nrm(ks[13], (DEPTH, D_FF), f32)
    w_ff2 = nrm(ks[14], (DEPTH, D_FF, D_MODEL), f32) * (D_FF ** -0.5) * BETA
    b_ff2 = 0.01 * nrm(ks[15], (DEPTH, D_MODEL), f32)
    ln2_gain = 1.0 + 0.01 * nrm(ks[16], (DEPTH, D_MODEL), f32)
    ln2_bias = 0.01 * nrm(ks[17], (DEPTH, D_MODEL), f32)
    return {"x": x, "w_in": w_in, "rel_bias": rel_bias, "ln_v_gain": ln_v_gain,
            "ln_v_bias": ln_v_bias, "w_spatial": w_spatial, "b_spatial": b_spatial,
            "w_proj_a": w_proj_a, "w_proj_b": w_proj_b, "w_out": w_out,
            "ln1_gain": ln1_gain, "ln1_bias": ln1_bias, "w_ff1": w_ff1, "b_ff1": b_ff1,
            "w_ff2": w_ff2, "b_ff2": b_ff2, "ln2_gain": ln2_gain, "ln2_bias": ln2_bias}


def reference(x, w_in, rel_bias, ln_v_gain, ln_v_bias, w_spatial, b_spatial,
              w_proj_a, w_proj_b, w_out, ln1_gain, ln1_bias, w_ff1, b_ff1,
              w_ff2, b_ff2, ln2_gain, ln2_bias):
    h = x
    for layer in range(DEPTH):
        mix = token_mixers(h, w_in[layer], rel_bias, ln_v_gain[layer], ln_v_bias[layer],
                           w_spatial[layer], b_spatial[layer], w_proj_a[layer],
                           w_proj_b[layer], w_out[layer])
        h = layer_norm(ALPHA * h + mix, ln1_gain[layer], ln1_bias[layer])
        ff = squared_relu_mlp(h, w_ff1[layer], b_ff1[layer], w_ff2[layer], b_ff2[layer])
        h = layer_norm(ALPHA * h + ff, ln2_gain[layer], ln2_bias[layer])
    return h
```

```python
import numpy as np
import concourse.bass as bass
import concourse.mybir as mybir
from concourse.bass_utils import run_bass_kernel_spmd

F32, BF16 = mybir.dt.float32, mybir.dt.bfloat16
AF = mybir.ActivationFunctionType
ALU = mybir.AluOpType

LN_EPS = 1e-5
PATTERNS = ((128, 1), (512, 4), (2048, 16))
N_BUCKETS = 32
MAX_DISTANCE = 2048
GELU_C = 1.5957691216057308
GELU_A = 0.044715


class Cfg:
    def __init__(self, D=2048, H=8, G=8, F=8192, T=2048, depth=1, stop_after=None):
        self.stop_after = stop_after
        self.D, self.H, self.G, self.F, self.T = D, H, G, F, T
        self.KD = D // 128
        self.DA = H * 128
        self.DB = G * 128
        self.KA, self.KB = H, G
        self.FC = F // 128
        self.DIN = 3 * self.DA + 2 * self.DB + 2 * D
        self.OQ, self.OK, self.OV = 0, self.DA, 2 * self.DA
        self.OU = 3 * self.DA
        self.OVB = self.OU + self.DB
        self.OGA = self.OVB + self.DB
        self.OGB = self.OGA + D
        self.TB = 512
        self.NB = T // self.TB
        self.NT = T // 128
        self.alpha = float((2 * depth) ** 0.25)
        self.V_G1, self.V_B1, self.V_BF2, self.V_G2, self.V_B2 = [i * self.KD for i in range(5)]
        self.V_BF1 = 5 * self.KD
        self.NV = 5 * self.KD + self.FC


class Res:
    __slots__ = ("w", "r", "excl")

    def __init__(self, excl=False):
        self.w = None
        self.r = {}
        self.excl = excl


class Eng:
    def __init__(self, name):
        self.name = name
        self.count = 0
        self.ops = []
        self.waited = {}
        self.sem = None


class DSem:
    def __init__(self, name):
        self.name = name
        self.count = 0
        self.sem = None


class Prog:
    def __init__(self):
        self.eng = {n: Eng(n) for n in ("pe", "act", "dve", "pool", "sp")}
        self.dsems = []

    def dsem(self, name):
        d = DSem(name)
        self.dsems.append(d)
        return d

    def _deps(self, e, reads, writes, dma_sem=None):
        need = {}

        def add(tok):
            if tok is None:
                return
            k, v = tok
            if e.name == "pe" and k is e:
                return
            if dma_sem is not None and k is dma_sem:
                return
            if e.waited.get(k, 0) >= v:
                return
            if need.get(k, 0) < v:
                need[k] = v

        for r in reads:
            add(r.w)
            if r.excl:
                for k, v in r.r.items():
                    if k is not e:
                        add((k, v))
        for w in writes:
            add(w.w)
            for k, v in w.r.items():
                add((k, v))
        for k, v in need.items():
            e.waited[k] = v
        return list(need.items())

    def op(self, en, fn, reads=(), writes=()):
        e = self.eng[en]
        waits = self._deps(e, reads, writes)
        e.count += 1
        for r in reads:
            r.r[e] = e.count
        for w in writes:
            w.w = (e, e.count)
            w.r = {}
        e.ops.append((waits, fn, (e, 1)))

    def dma(self, en, fn, dsem, reads=(), writes=(), extra=()):
        e = self.eng[en]
        waits = self._deps(e, reads, writes, dma_sem=dsem)
        for k, v in extra:
            if e.waited.get(k, 0) < v:
                waits.append((k, v))
                e.waited[k] = v
        kind = "sw" if en == "pool" else "hw"
        assert getattr(dsem, "kind", kind) == kind, dsem.name
        dsem.kind = kind
        dsem.count += 16
        for r in reads:
            r.r[dsem] = dsem.count
        for w in writes:
            w.w = (dsem, dsem.count)
            w.r = {}
        e.ops.append((waits, fn, (dsem, 16)))

    def barrier(self):
        toks = [(e, e.count) for e in self.eng.values() if e.count > 0]
        toks += [(d, d.count) for d in self.dsems if d.count > 0]
        for e in self.eng.values():
            waits = []
            for k, v in toks:
                if k is e:
                    continue
                if e.waited.get(k, 0) < v:
                    waits.append((k, v))
                    e.waited[k] = v
            if waits:
                e.ops.append((waits, None, None))

    def final_wait(self, en, dsems):
        e = self.eng[en]
        e.ops.append(([(d, d.count) for d in dsems if d.count > 0], None, None))

    def emit(self, nc, stack):
        for e in self.eng.values():
            e.sem = stack.enter_context(nc.semaphore("s_" + e.name))
        for d in self.dsems:
            d.sem = stack.enter_context(nc.semaphore("d_" + d.name))
        block = stack.enter_context(nc.Block())

        def mk(e):
            def body(engine):
                for waits, fn, inc in e.ops:
                    for k, v in waits:
                        engine.wait_ge(k.sem, v)
                    if fn is not None:
                        inst = fn(engine)
                        inst.then_inc(inc[0].sem, inc[1])
            return body

        block.tensor(mk(self.eng["pe"]))
        block.scalar(mk(self.eng["act"]))
        block.vector(mk(self.eng["dve"]))
        block.gpsimd(mk(self.eng["pool"]))
        block.sync(mk(self.eng["sp"]))


class Rot:
    def __init__(self, items):
        self.items = list(items)
        self.i = 0

    def next(self):
        x = self.items[self.i % len(self.items)]
        self.i += 1
        return x


class WStream:
    SLOT_ELEMS = 8192

    def __init__(self, prog, nc, nslots, base_off, plan):
        self.prog = prog
        self.nslots = nslots
        self.slots = [nc.alloc_sbuf_tensor_at("wslot%d" % i, [128, self.SLOT_ELEMS], BF16,
                                              offset=base_off + i * self.SLOT_ELEMS * 2)
                      for i in range(nslots)]
        self.res = [Res() for _ in range(nslots)]
        self.dsem = [prog.dsem("w%d" % i) for i in range(nslots)]
        self.plan = plan
        self.rec = []
        self.idx = 0
        self.loaded = 0

    def _views(self, slot, spec):
        _, _, KC, segs = spec
        out = []
        off = 0
        for (_, width) in segs:
            v = self.slots[slot][:, off:off + KC * width].rearrange("p (k w) -> p k w", k=KC)
            out.append(v)
            off += KC * width
        assert off <= self.SLOT_ELEMS
        return out

    def _load(self, i, spec):
        slot = i % self.nslots
        dram, row0, KC, segs = spec
        views = self._views(slot, spec)
        extra = [self.last_tok] if getattr(self, "last_tok", None) is not None else []
        for (col0, width), v in zip(segs, views):
            src = dram[row0:row0 + KC * 128, col0:col0 + width].rearrange("(k p) w -> p k w", p=128)
            step = 4
            for k0 in range(0, KC, step):
                k1 = min(KC, k0 + step)
                self.prog.dma("pool",
                              (lambda g, o=v[:, k0:k1, :], s=src[:, k0:k1, :]: g.dma_start(out=o, in_=s)),
                              self.dsem[slot], writes=[self.res[slot]], extra=extra)
                extra = []
        self.last_tok = (self.dsem[slot], self.dsem[slot].count)

    def get(self, spec, held=0):
        i = self.idx
        self.idx += 1
        if self.plan is None:
            self.rec.append(spec)
            target = i
        else:
            target = min(i + self.nslots - 1 - held, len(self.plan) - 1)
        while self.loaded <= target:
            self._load(self.loaded, self.plan[self.loaded] if self.plan is not None else spec)
            self.loaded += 1
        slot = i % self.nslots
        return self._views(slot, spec), self.res[slot]


def build_program(nc, cfg, prog, plan):
    C = cfg
    D, H, G, T, KD, FC, TB = C.D, C.H, C.G, C.T, C.KD, C.FC, C.TB
    NTB = T // 512

    def din(name, shape):
        return nc.dram_tensor(name, list(shape), F32, kind="ExternalInput").ap()

    x_d = din("x", [T, D])
    w_in_d = din("w_in", [D, C.DIN])
    w_pa_d = din("w_proj_a", [C.DA, D])
    w_pb_d = din("w_proj_b", [C.DB, D])
    w_out_d = din("w_out", [D, D])
    w_ff1_d = din("w_ff1", [D, C.F])
    w_ff2_d = din("w_ff2", [C.F, D])
    w_sp_d = din("w_sp", [G, 128, 128])
    vecs_d = din("vecs", [128, C.NV])
    lnv_d = din("lnv", [2, 128, C.DB])
    bsp_d = din("bsp", [128, G * 128])
    btab_d = din("btab", [128, 3 * H * 256])
    mask_d = din("mask", [128, 256])
    ident_d = din("ident", [128, 128])
    out_d = nc.dram_tensor("out", [T, D], F32, kind="ExternalOutput").ap()
    attn_s = nc.dram_tensor("attn_s", [H, 128, T], BF16).ap()
    gmlp_s = nc.dram_tensor("gmlp_s", [G, 128, T], BF16).ap()
    attn_s_res = [Res() for _ in range(H)]
    gmlp_s_res = [Res() for _ in range(G)]

    SB_BASE = 16640
    LIMIT = 224 * 1024

    class Bump:
        def __init__(self, base):
            self.off = base

        def alloc(self, name, shape, dt):
            nbytes = int(np.prod(shape[1:])) * (2 if dt == BF16 else 4)
            nbytes = (nbytes + 31) // 32 * 32
            t = nc.alloc_sbuf_tensor_at(name, list(shape), dt, offset=self.off)
            self.off += nbytes
            assert self.off <= LIMIT, (name, self.off)
            return t

    pb = Bump(SB_BASE)
    ident = pb.alloc("ident", [128, 128], BF16)
    ones = pb.alloc("ones", [128, 128], BF16)
    mask = pb.alloc("mask", [128, 256], F32)
    vecs = pb.alloc("vecs", [128, C.NV], F32)
    r_const = Res()
    r_ident = Res()
    W = WStream(prog, nc, 3, pb.off, plan)
    pb.off += 3 * WStream.SLOT_ELEMS * 2
    P_BASE = pb.off

    psum = [nc.alloc_psum_tensor("ps%d" % i, [128, 512], F32) for i in range(8)]
    pres = [Res(excl=True) for _ in range(8)]

    d_const = prog.dsem("const")
    d_ident = prog.dsem("ident")
    prog.dma("pool", lambda g: g.dma_start(out=ident[:, :], in_=ident_d[:, :]), d_ident, writes=[r_ident])
    prog.dma("sp", lambda g: g.dma_start(out=mask[:, :], in_=mask_d[:, :]), d_const, writes=[r_const])
    prog.dma("sp", lambda g: g.dma_start(out=vecs[:, :], in_=vecs_d[:, :]), d_const, writes=[r_const])
    r_ones = Res()
    prog.op("pool", lambda g: g.memset(ones[:, :], 1.0), writes=[r_ones])

    def vcol(i):
        return vecs[:, i:i + 1]

    evac_rot = Rot(["act", "dve"])

    def copy_op(en, out, in_, reads, writes):
        if en == "act":
            prog.op("act", lambda a: a.activation(out=out, in_=in_, func=AF.Copy), reads=reads, writes=writes)
        elif en == "dve":
            prog.op("dve", lambda v: v.tensor_copy(out=out, in_=in_), reads=reads, writes=writes)
        else:
            prog.op("pool", lambda g: g.tensor_copy(out=out, in_=in_), reads=reads, writes=writes)

    p1 = Bump(P_BASE)
    xT = p1.alloc("xT", [128, KD, T], BF16)
    r_xT = [Res() for _ in range(NTB)]
    R_BASE = p1.off

    rb = Bump(R_BASE)
    xb = rb.alloc("xb", [128, 4, D], BF16)
    r_xb = Res()
    d_xb = prog.dsem("xb")
    aux = Rot([4, 5, 6, 7])
    for tg in range(NTB):
        for t4 in range(4):
            t = tg * 4 + t4
            prog.dma("pool", lambda g, t=t, t4=t4: g.dma_start(out=xb[:, t4, :], in_=x_d[t * 128:(t + 1) * 128, :]),
                     d_xb, writes=[r_xb])
        en = "act" if tg % 2 == 0 else "dve"
        for c in range(KD):
            b = aux.next()

            def fn(pe, b=b, c=c):
                last = None
                for t4 in range(4):
                    last = pe.matmul(psum[b][:, t4 * 128:(t4 + 1) * 128], xb[:, t4, c * 128:(c + 1) * 128],
                                     ident[:, :], start=True, stop=True)
                return last
            prog.op("pe", fn, reads=[r_xb, r_ident], writes=[pres[b]])
            copy_op(en, xT[:, c, tg * 512:(tg + 1) * 512], psum[b][:, :], [pres[b]], [r_xT[tg]])

    if C.stop_after == "xT":
        return W
    prog.barrier()
    rb = Bump(R_BASE)
    vbn = rb.alloc("vbn", [128, C.NT, C.DB], BF16)
    r_vbn = [Res() for _ in range(C.NT)]
    uT = rb.alloc("uT", [128, 4, T], BF16)
    r_uT = Res()
    lnv = rb.alloc("lnv", [128, 2, C.DB], F32)
    bsp = rb.alloc("bspt", [128, G, 128], F32)
    wsT = rb.alloc("wsT", [128, G, 128], BF16)
    wsl = rb.alloc("wsl", [128, G, 128], BF16)
    r_tab = Res()
    r_wsl = Res()
    r_wsT = Res()
    gx2 = [rb.alloc("gx2_%d" % i, [128, 512], F32) for i in range(2)]
    gt = [rb.alloc("gt_%d" % i, [128, 512], F32) for i in range(2)]
    r_gx2 = [Res(), Res()]
    r_gt = [Res(), Res()]
    vbg = [rb.alloc("vbg_%d" % i, [128, C.DB], F32) for i in range(2)]
    r_vbg = [Res(), Res()]
    gm = rb.alloc("gm", [128, T], BF16)
    r_gm = Res()
    stats = [rb.alloc("bnst_%d" % i, [128, 4, 6], F32) for i in range(2)]
    mv = [rb.alloc("bnmv_%d" % i, [128, 4], F32) for i in range(2)]
    r_mv = [Res(), Res()]

    d_tab = prog.dsem("tab")
    prog.dma("sp", lambda g: g.dma_start(out=lnv[:, 0, :], in_=lnv_d[0, :, :]), d_tab, writes=[r_tab])
    prog.dma("sp", lambda g: g.dma_start(out=lnv[:, 1, :], in_=lnv_d[1, :, :]), d_tab, writes=[r_tab])
    prog.dma("sp", lambda g: g.dma_start(out=bsp[:, :, :], in_=bsp_d.rearrange("p (g i) -> p g i", g=G)),
             d_tab, writes=[r_tab])
    d_wsl = prog.dsem("wsl")
    for g_ in range(G):
        prog.dma("pool", lambda g, g_=g_: g.dma_start(out=wsl[:, g_, :], in_=w_sp_d[g_, :, :]), d_wsl, writes=[r_wsl])
    for g0 in range(0, G, 4):
        b = aux.next()
        ng = min(4, G - g0)

        def fn(pe, b=b, g0=g0, ng=ng):
            last = None
            for gi in range(ng):
                last = pe.matmul(psum[b][:, gi * 128:(gi + 1) * 128], wsl[:, g0 + gi, :], ident[:, :],
                                 start=True, stop=True)
            return last
        prog.op("pe", fn, reads=[r_wsl, r_ident], writes=[pres[b]])
        for gi in range(ng):
            prog.op("dve", lambda v, b=b, gi=gi, g0=g0: v.tensor_tensor(
                out=wsT[:, g0 + gi, :], in0=psum[b][:, gi * 128:(gi + 1) * 128], in1=mask[:, 0:128], op=ALU.mult),
                reads=[pres[b], r_const], writes=[r_wsT])

    gel_i = [0]

    def gelu_from_psum(b, out_ap, out_res, width=512):
        i = gel_i[0] % 2
        gel_i[0] += 1
        pv = psum[b][:, 0:width]
        x2, tt = gx2[i][:, 0:width], gt[i][:, 0:width]
        prog.op("act", lambda a: a.activation(out=x2, in_=pv, func=AF.Square), reads=[pres[b]], writes=[r_gx2[i]])
        prog.op("dve", lambda v: v.tensor_scalar(out=tt, in0=x2, scalar1=GELU_C * GELU_A, scalar2=GELU_C,
                                                 op0=ALU.mult, op1=ALU.add), reads=[r_gx2[i]], writes=[r_gt[i]])
        prog.op("dve", lambda v: v.tensor_tensor(out=tt, in0=tt, in1=pv, op=ALU.mult),
                reads=[r_gt[i], pres[b]], writes=[r_gt[i]])
        prog.op("act", lambda a: a.activation(out=x2, in_=tt, func=AF.Sigmoid), reads=[r_gt[i]], writes=[r_gx2[i]])
        prog.op("dve", lambda v: v.tensor_tensor(out=out_ap, in0=x2, in1=pv, op=ALU.mult),
                reads=[r_gx2[i], pres[b]], writes=[out_res])

    nvu = (C.DB + 511) // 512
    vb_units = []
    for vg in range(nvu):
        wdt = min(512, C.DB - vg * 512)
        views, wres = W.get((w_in_d, 0, KD, [(C.OVB + vg * 512, wdt)]), held=vg)
        vb_units.append((views[0], wres, wdt))
    gb = Rot([0, 1, 2, 3])
    for t in range(C.NT):
        i2 = t % 2
        tg = t // 4
        for vg, (wv, wres, wdt) in enumerate(vb_units):
            b = gb.next()

            def fn(pe, b=b, t=t, wv=wv, wdt=wdt):
                last = None
                for k in range(KD):
                    last = pe.matmul(psum[b][:, 0:wdt], xT[:, k, t * 128:(t + 1) * 128], wv[:, k, :],
                                     start=(k == 0), stop=(k == KD - 1))
                return last
            prog.op("pe", fn, reads=[r_xT[tg], wres], writes=[pres[b]])
            gelu_from_psum(b, vbg[i2][:, vg * 512:vg * 512 + wdt], r_vbg[i2], wdt)
        nch = (C.DB + 511) // 512

        def fst(v, i2=i2, nch=nch):
            last = None
            for ch in range(nch):
                w_ = min(512, C.DB - ch * 512)
                last = v.bn_stats(out=stats[i2][:, ch, :], in_=vbg[i2][:, ch * 512:ch * 512 + w_])
            return last
        prog.op("dve", fst, reads=[r_vbg[i2]], writes=[r_mv[i2]])
        prog.op("dve", lambda v, i2=i2, nch=nch: v.bn_aggr(out=mv[i2][:, 0:2], in_=stats[i2][:, 0:nch, :]),
                reads=[r_mv[i2]], writes=[r_mv[i2]])
        prog.op("act", lambda a, i2=i2: a.activation(out=mv[i2][:, 2:3], in_=mv[i2][:, 1:2], func=AF.Sqrt,
                                                     bias=float(LN_EPS), scale=1.0),
                reads=[r_mv[i2]], writes=[r_mv[i2]])
        prog.op("dve", lambda v, i2=i2: v.reciprocal(out=mv[i2][:, 3:4], in_=mv[i2][:, 2:3]),
                reads=[r_mv[i2]], writes=[r_mv[i2]])
        prog.op("dve", lambda v, i2=i2: v.tensor_scalar(out=vbg[i2][:, :], in0=vbg[i2][:, :], scalar1=mv[i2][:, 0:1],
                                                        scalar2=mv[i2][:, 3:4], op0=ALU.subtract, op1=ALU.mult),
                reads=[r_mv[i2], r_vbg[i2]], writes=[r_vbg[i2]])
        prog.op("dve", lambda v, i2=i2: v.tensor_tensor(out=vbg[i2][:, :], in0=vbg[i2][:, :], in1=lnv[:, 0, :],
                                                        op=ALU.mult), reads=[r_vbg[i2], r_tab], writes=[r_vbg[i2]])
        prog.op("dve", lambda v, i2=i2, t=t: v.tensor_tensor(out=vbn[:, t, :], in0=vbg[i2][:, :], in1=lnv[:, 1, :],
                                                             op=ALU.add), reads=[r_vbg[i2], r_tab], writes=[r_vbn[t]])

    d_gm = prog.dsem("gm")
    for ug in range((G + 3) // 4):
        ng = min(4, G - ug * 4)
        views, wres = W.get((w_in_d, 0, KD, [(C.OU + ug * 512, ng * 128)]))
        wu = views[0]
        for gi in range(ng):
            g_ = ug * 4 + gi

            def fn(pe, wu=wu, gi=gi):
                last = None
                for k in range(KD):
                    for tb in range(NTB):
                        last = pe.matmul(psum[tb][:, :], wu[:, k, gi * 128:(gi + 1) * 128],
                                         xT[:, k, tb * 512:(tb + 1) * 512], start=(k == 0), stop=(k == KD - 1))
                return last
            prog.op("pe", fn, reads=r_xT + [wres], writes=[pres[tb] for tb in range(NTB)])
            for tb in range(NTB):
                gelu_from_psum(tb, uT[:, gi, tb * 512:(tb + 1) * 512], r_uT)
            for tb in range(NTB):
                b = aux.next()

                def fz(pe, b=b, tb=tb, g_=g_):
                    last = None
                    for c4 in range(4):
                        t = tb * 4 + c4
                        last = pe.matmul(psum[b][:, c4 * 128:(c4 + 1) * 128], vbn[:, t, g_ * 128:(g_ + 1) * 128],
                                         wsT[:, g_, :], start=True, stop=True)
                    return last
                prog.op("pe", fz, reads=[r_vbn[tb * 4 + c4] for c4 in range(4)] + [r_wsT], writes=[pres[b]])
                i = gel_i[0] % 2
                gel_i[0] += 1
                zt = gt[i]
                prog.op("dve", lambda v, b=b, zt=zt, g_=g_: v.tensor_tensor(
                    out=zt[:, :].rearrange("p (c i) -> p c i", c=4),
                    in0=psum[b][:, :].rearrange("p (c i) -> p c i", c=4),
                    in1=bsp[:, g_:g_ + 1, :].to_broadcast([128, 4, 128]), op=ALU.add),
                    reads=[pres[b], r_tab], writes=[r_gt[i]])
                prog.op("dve", lambda v, zt=zt, gi=gi, tb=tb: v.tensor_tensor(
                    out=gm[:, tb * 512:(tb + 1) * 512], in0=zt[:, :], in1=uT[:, gi, tb * 512:(tb + 1) * 512],
                    op=ALU.mult), reads=[r_gt[i], r_uT], writes=[r_gm])
            prog.dma("sp", lambda s, g_=g_: s.dma_start(out=gmlp_s[g_, :, :], in_=gm[:, :]), d_gm,
                     reads=[r_gm], writes=[gmlp_s_res[g_]])

    if C.stop_after == "1a":
        return W
    prog.barrier()
    rb = Bump(R_BASE)
    ebt = rb.alloc("ebt", [128, 3 * H, 256], F32)
    r_ebt = Res()
    qkv = [rb.alloc("qkv%d" % i, [128, T], BF16) for i in range(3)]
    r_qkv = [Res(), Res(), Res()]
    Vp = [rb.alloc("Vp%d" % i, [128, 16, 128], BF16) for i in range(3)]
    r_Vp = [Res(), Res(), Res()]
    acc = rb.alloc("acc", [128, 2, T], F32)
    r_acc = Res()
    esb = [rb.alloc("esb%d" % i, [128, 256], F32) for i in range(2)]
    r_esb = [Res(), Res()]
    ptb = [rb.alloc("ptb%d" % i, [128, 256], BF16) for i in range(2)]
    r_ptb = [Res(), Res()]
    ast = rb.alloc("ast", [128, T], BF16)
    r_ast = Res()

    d_bt = prog.dsem("btab")
    prog.dma("sp", lambda s: s.dma_start(out=ebt[:, :, :], in_=btab_d.rearrange("p (a b) -> p a b", b=256)),
             d_bt, writes=[r_ebt])
    prog.op("act", lambda a: a.activation(out=ebt[:, :, :], in_=ebt[:, :, :], func=AF.Exp), reads=[r_ebt], writes=[r_ebt])
    prog.op("dve", lambda v: v.tensor_tensor(out=ebt[:, :, :], in0=ebt[:, :, :],
                                             in1=mask[:, :].unsqueeze(1).to_broadcast([128, 3 * H, 256]), op=ALU.mult),
            reads=[r_ebt, r_const], writes=[r_ebt])

    d_ast = prog.dsem("ast")
    scb = Rot([4, 5])
    ob = Rot([6, 7])
    scale = float(128 ** -0.5)
    u_i = 0
    for h in range(H):
        views, wres = W.get((w_in_d, 0, KD, [(C.OQ + h * 128, 128), (C.OK + h * 128, 128), (C.OV + h * 128, 128)]))
        for j in range(3):
            wv = views[j]

            def fn(pe, wv=wv):
                last = None
                for k in range(KD):
                    for tb in range(NTB):
                        last = pe.matmul(psum[tb][:, :], wv[:, k, :], xT[:, k, tb * 512:(tb + 1) * 512],
                                         start=(k == 0), stop=(k == KD - 1))
                return last
            prog.op("pe", fn, reads=r_xT + [wres], writes=[pres[tb] for tb in range(NTB)])
            for tb in range(NTB):
                copy_op("act" if tb % 2 == 0 else "dve", qkv[j][:, tb * 512:(tb + 1) * 512], psum[tb][:, :],
                        [pres[tb]], [r_qkv[j]])
        qT, kT, vT = qkv
        for p, (_, dil) in enumerate(PATTERNS):
            nblk = T // dil // 128
            for q4 in range(4):
                b = scb.next() if q4 % 2 == 0 else ob.next()

                def fv(pe, b=b, p=p, dil=dil, nblk=nblk, q4=q4):
                    last = None
                    for i4 in range(4):
                        blk = q4 * 4 + i4
                        r, n = blk // nblk, blk % nblk
                        st = r + dil * 128 * n
                        last = pe.matmul(psum[b][:, i4 * 128:(i4 + 1) * 128], vT[:, st:st + dil * 127 + 1:dil],
                                         ident[:, :], start=True, stop=True)
                    return last
                prog.op("pe", fv, reads=[r_qkv[2], r_ident], writes=[pres[b]])
                copy_op("act" if q4 % 2 == 0 else "dve",
                        Vp[p][:, q4 * 4:(q4 + 1) * 4, :], psum[b][:, :].rearrange("p (a b) -> p a b", a=4),
                        [pres[b]], [r_Vp[p]])
        for p, (_, dil) in enumerate(PATTERNS):
            nblk = T // dil // 128
            tab = ebt[:, p * H + h, :]
            for r in range(dil):
                for n in range(nblk):
                    blk = r * nblk + n
                    st = r + dil * 128 * n
                    toks = slice(st, st + dil * 127 + 1, dil)
                    hp = n > 0
                    ptoks = slice(st - dil * 128, st - dil * 128 + dil * 127 + 1, dil)
                    wd = 256 if hp else 128
                    sb_ = scb.next()
                    o_ = ob.next()
                    i2 = u_i % 2
                    u_i += 1

                    def fs(pe, sb_=sb_, toks=toks, ptoks=ptoks, hp=hp):
                        last = pe.matmul(psum[sb_][:, 0:128], kT[:, toks], qT[:, toks], start=True, stop=True)
                        if hp:
                            last = pe.matmul(psum[sb_][:, 128:256], kT[:, ptoks], qT[:, toks], start=True, stop=True)
                        return last
                    prog.op("pe", fs, reads=[r_qkv[0], r_qkv[1]], writes=[pres[sb_]])
                    prog.op("act", lambda a, sb_=sb_, i2=i2, wd=wd: a.activation(
                        out=esb[i2][:, 0:wd], in_=psum[sb_][:, 0:wd], func=AF.Exp, scale=scale),
                        reads=[pres[sb_]], writes=[r_esb[i2]])
                    prog.op("pool", lambda g, i2=i2, wd=wd, tab=tab: g.tensor_tensor(
                        out=ptb[i2][:, 0:wd], in0=esb[i2][:, 0:wd], in1=tab[:, 0:wd], op=ALU.mult),
                        reads=[r_esb[i2], r_ebt], writes=[r_ptb[i2]])

                    def fo(pe, o_=o_, i2=i2, p=p, blk=blk, hp=hp):
                        pe.matmul(psum[o_][:, 0:128], Vp[p][:, blk, :], ptb[i2][:, 0:128], start=True, stop=not hp)
                        if hp:
                            pe.matmul(psum[o_][:, 0:128], Vp[p][:, blk - 1, :], ptb[i2][:, 128:256], start=False, stop=True)
                        last = pe.matmul(psum[o_][:, 128:256], ones[:, :], ptb[i2][:, 0:128], start=True, stop=not hp)
                        if hp:
                            last = pe.matmul(psum[o_][:, 128:256], ones[:, :], ptb[i2][:, 128:256], start=False, stop=True)
                        return last
                    prog.op("pe", fo, reads=[r_ptb[i2], r_Vp[p], r_ones], writes=[pres[o_]])
                    ov = psum[o_][:, 0:256].rearrange("p (a b) -> p a b", a=2)
                    av = acc[:, :, toks]
                    if p == 0:
                        prog.op("dve", lambda v, av=av, ov=ov: v.tensor_copy(out=av, in_=ov),
                                reads=[pres[o_]], writes=[r_acc])
                    else:
                        prog.op("dve", lambda v, av=av, ov=ov: v.tensor_tensor(out=av, in0=av, in1=ov, op=ALU.add),
                                reads=[pres[o_], r_acc], writes=[r_acc])
        prog.op("dve", lambda v: v.reciprocal(out=acc[:, 1, :], in_=acc[:, 1, :]), reads=[r_acc], writes=[r_acc])
        prog.op("dve", lambda v: v.tensor_tensor(out=ast[:, :], in0=acc[:, 0, :], in1=acc[:, 1, :], op=ALU.mult),
                reads=[r_acc], writes=[r_ast])
        prog.dma("sp", lambda s, h=h: s.dma_start(out=attn_s[h, :, :], in_=ast[:, :]), d_ast,
                 reads=[r_ast], writes=[attn_s_res[h]])

    if C.stop_after == "1b":
        return W
    prog.barrier()
    p2 = Bump(P_BASE)
    x32 = p2.alloc("x32", [128, KD, TB], F32)
    r_x32 = [Res() for _ in range(KD)]
    xT2 = p2.alloc("xT2", [128, KD, TB], BF16)
    r_xT2 = [Res() for _ in range(KD)]
    ARENA = p2.off
    ARENA_SZ = max(64 * 1024, FC * TB * 2)
    p2.off += ARENA_SZ
    r_xtok, r_xhi, r_xlo, r_attn_j, r_gmlp_j, r_otok = Res(), Res(), Res(), Res(), Res(), Res()
    r_merged = [Res() for _ in range(KD)]
    r_sga = [Res() for _ in range(4)]
    r_sgb = [Res() for _ in range(4)]
    r_aT = [Res() for _ in range(FC)]
    r_piece = [Res() for _ in range(KD)]
    r_rr = [Res(), Res()]
    hb = [p2.alloc("hb%d" % i, [128, TB], BF16) for i in range(2)]
    sq = [p2.alloc("sq%d" % i, [128, TB], BF16) for i in range(2)]
    r_hb = [Res(), Res()]
    r_sq = [Res(), Res()]
    mean = p2.alloc("mean", [128, TB], F32)
    rstd = p2.alloc("rstd", [128, TB], F32)
    ex2 = p2.alloc("ex2", [128, TB], F32)
    r_st = Res()
    tt2 = [p2.alloc("tt%d" % i, [128, TB], F32) for i in range(2)]
    r_tt2 = [Res(), Res()]
    rl = [p2.alloc("rl%d" % i, [128, TB], F32) for i in range(2)]
    r_rl = [Res(), Res()]

    def arena_alloc(name, shape, dt, off):
        nbytes = int(np.prod(shape[1:])) * (2 if dt == BF16 else 4)
        assert off + nbytes <= ARENA_SZ, (name, off, nbytes)
        return nc.alloc_sbuf_tensor_at(name, list(shape), dt, offset=ARENA + off)

    xtok = arena_alloc("xtok", [128, 4, D], F32, 0)
    xhi = arena_alloc("xhi", [128, 4, D], BF16, 4 * D * 4)
    xlo = arena_alloc("xlo", [128, 4, D], BF16, 4 * D * 6)
    attn_j = arena_alloc("attn_j", [128, C.KA, TB], BF16, 0)
    gmlp_j = arena_alloc("gmlp_j", [128, C.KB, TB], BF16, C.KA * TB * 2)
    o2 = (C.KA + C.KB) * TB * 2
    merged = arena_alloc("merged", [128, KD, TB], BF16, o2)
    o2 += KD * TB * 2
    sga = arena_alloc("sga", [128, 4, TB], F32, o2)
    sgb = arena_alloc("sgb", [128, 4, TB], F32, o2 + 4 * TB * 4)
    aT = arena_alloc("aT", [128, FC, TB], BF16, 0)
    pieces = [arena_alloc("piece%d" % i, [128, KD, TB], BF16, i * KD * TB * 2) for i in range(3)]
    o6 = 3 * KD * TB * 2
    otok = arena_alloc("otok", [128, D], F32, o6)
    rr = [arena_alloc("rr%d" % i, [128, TB], F32, o6 + D * 4 + i * TB * 4) for i in range(2)]

    d_x = prog.dsem("xtok")
    d_aj = prog.dsem("aj")
    d_gj = prog.dsem("gj")
    d_out = prog.dsem("out")
    g4 = Rot([0, 1, 2, 3])
    s67 = Rot([6, 7])
    alpha = C.alpha
    invD = 1.0 / D

    def ln_stats_chunk(c, i2):
        prog.op("act", lambda a, c=c, i2=i2: a.activation(out=hb[i2][:, :], in_=x32[:, c, :], func=AF.Copy),
                reads=[r_x32[c]], writes=[r_hb[i2]])
        prog.op("act", lambda a, c=c, i2=i2: a.activation(out=sq[i2][:, :], in_=x32[:, c, :], func=AF.Square),
                reads=[r_x32[c]], writes=[r_sq[i2]])

        def fn(pe, c=c, i2=i2):
            pe.matmul(psum[4][:, :], ones[:, :], hb[i2][:, :], start=(c == 0), stop=(c == KD - 1))
            return pe.matmul(psum[5][:, :], ones[:, :], sq[i2][:, :], start=(c == 0), stop=(c == KD - 1))
        prog.op("pe", fn, reads=[r_hb[i2], r_sq[i2], r_ones], writes=[pres[4], pres[5]])

    def ln_finish_stats():
        prog.op("dve", lambda v: v.tensor_scalar(out=mean[:, :], in0=psum[4][:, :], scalar1=invD, scalar2=None,
                                                 op0=ALU.mult), reads=[pres[4]], writes=[r_st])
        prog.op("dve", lambda v: v.tensor_scalar(out=ex2[:, :], in0=psum[5][:, :], scalar1=invD, scalar2=None,
                                                 op0=ALU.mult), reads=[pres[5], r_st], writes=[r_st])
        prog.op("dve", lambda v: v.tensor_tensor(out=rstd[:, :], in0=mean[:, :], in1=mean[:, :], op=ALU.mult),
                reads=[r_st], writes=[r_st])
        prog.op("dve", lambda v: v.tensor_tensor(out=ex2[:, :], in0=ex2[:, :], in1=rstd[:, :], op=ALU.subtract),
                reads=[r_st], writes=[r_st])
        prog.op("act", lambda a: a.activation(out=ex2[:, :], in_=ex2[:, :], func=AF.Sqrt, bias=float(LN_EPS), scale=1.0),
                reads=[r_st], writes=[r_st])
        prog.op("dve", lambda v: v.reciprocal(out=rstd[:, :], in_=ex2[:, :]), reads=[r_st], writes=[r_st])

    def ln_apply_chunk(c, i2, gcol, bcol):
        prog.op("dve", lambda v, c=c, i2=i2: v.tensor_tensor(out=tt2[i2][:, :], in0=x32[:, c, :], in1=mean[:, :],
                                                             op=ALU.subtract), reads=[r_x32[c], r_st], writes=[r_tt2[i2]])
        prog.op("dve", lambda v, i2=i2: v.tensor_tensor(out=tt2[i2][:, :], in0=tt2[i2][:, :], in1=rstd[:, :],
                                                        op=ALU.mult), reads=[r_tt2[i2], r_st], writes=[r_tt2[i2]])
        prog.op("dve", lambda v, c=c, i2=i2: v.tensor_scalar(out=x32[:, c, :], in0=tt2[i2][:, :], scalar1=vcol(gcol + c),
                                                             scalar2=vcol(bcol + c), op0=ALU.mult, op1=ALU.add),
                reads=[r_tt2[i2], r_const], writes=[r_x32[c]])

    for j in range(C.NB):
        t0 = j * TB
        prog.barrier()
        for t4 in range(4):
            prog.dma("pool", lambda s, t4=t4, t0=t0: s.dma_start(out=xtok[:, t4, :],
                                                                  in_=x_d[t0 + t4 * 128:t0 + (t4 + 1) * 128, :]),
                     d_x, writes=[r_xtok])
        for t4 in range(4):
            prog.op("act", lambda a, t4=t4: a.activation(out=xhi[:, t4, :], in_=xtok[:, t4, :], func=AF.Copy),
                    reads=[r_xtok], writes=[r_xhi])
        for t4 in range(4):
            prog.op("dve", lambda v, t4=t4: v.tensor_tensor(out=xtok[:, t4, :], in0=xtok[:, t4, :], in1=xhi[:, t4, :],
                                                            op=ALU.subtract),
                    reads=[r_xtok, r_xhi], writes=[r_xtok])
        for t4 in range(4):
            prog.op("act", lambda a, t4=t4: a.activation(out=xlo[:, t4, :], in_=xtok[:, t4, :], func=AF.Copy),
                    reads=[r_xtok], writes=[r_xlo])
        for c in range(KD):
            b = s67.next()

            def fn(pe, b=b, c=c):
                last = None
                for t4 in range(4):
                    pe.matmul(psum[b][:, t4 * 128:(t4 + 1) * 128], xhi[:, t4, c * 128:(c + 1) * 128], ident[:, :],
                              start=True, stop=False)
                    last = pe.matmul(psum[b][:, t4 * 128:(t4 + 1) * 128], xlo[:, t4, c * 128:(c + 1) * 128], ident[:, :],
                                     start=False, stop=True)
                return last
            prog.op("pe", fn, reads=[r_xhi, r_xlo, r_ident], writes=[pres[b]])
            copy_op("dve", x32[:, c, :], psum[b][:, :], [pres[b]], [r_x32[c]])
            copy_op("act", xT2[:, c, :], psum[b][:, :], [pres[b]], [r_xT2[c]])
        if C.stop_after == "S1":
            return W
        prog.barrier()
        prog.dma("pool", lambda s, t0=t0: s.dma_start(out=attn_j[:, :, :],
                                                      in_=attn_s[:, :, t0:t0 + TB].rearrange("h p t -> p h t")),
                 d_aj, reads=attn_s_res, writes=[r_attn_j])
        prog.dma("pool", lambda s, t0=t0: s.dma_start(out=gmlp_j[:, :, :],
                                                      in_=gmlp_s[:, :, t0:t0 + TB].rearrange("g p t -> p g t")),
                 d_gj, reads=gmlp_s_res, writes=[r_gmlp_j])
        for cg in range((KD + 3) // 4):
            nch = min(4, KD - cg * 4)
            wdt = nch * 128

            def gemm1(wv, KC, src, src_res, ci, b):
                def fn(pe):
                    last = None
                    for k in range(KC):
                        last = pe.matmul(psum[b][:, :], wv[:, k, ci * 128:(ci + 1) * 128], src[:, k, :],
                                         start=(k == 0), stop=(k == KC - 1))
                    return last
                prog.op("pe", fn, reads=src_res, writes=[pres[b]])

            views, wres = W.get((w_in_d, 0, KD, [(C.OGA + cg * 512, wdt)]))
            for ci in range(nch):
                b = g4.next()
                gemm1(views[0], KD, xT2, r_xT2 + [wres], ci, b)
                prog.op("act", lambda a, ci=ci, b=b: a.activation(out=sga[:, ci, :], in_=psum[b][:, :], func=AF.Sigmoid),
                        reads=[pres[b]], writes=[r_sga[ci]])
            views, wres = W.get((w_pa_d, 0, C.KA, [(cg * 512, wdt)]))
            for ci in range(nch):
                b = g4.next()
                gemm1(views[0], C.KA, attn_j, [r_attn_j, wres], ci, b)
                prog.op("dve", lambda v, ci=ci, b=b: v.tensor_tensor(out=sga[:, ci, :], in0=sga[:, ci, :],
                                                                     in1=psum[b][:, :], op=ALU.mult),
                        reads=[pres[b], r_sga[ci]], writes=[r_sga[ci]])
            views, wres = W.get((w_in_d, 0, KD, [(C.OGB + cg * 512, wdt)]))
            for ci in range(nch):
                b = g4.next()
                gemm1(views[0], KD, xT2, r_xT2 + [wres], ci, b)
                prog.op("act", lambda a, ci=ci, b=b: a.activation(out=sgb[:, ci, :], in_=psum[b][:, :], func=AF.Sigmoid),
                        reads=[pres[b]], writes=[r_sgb[ci]])
            views, wres = W.get((w_pb_d, 0, C.KB, [(cg * 512, wdt)]))
            for ci in range(nch):
                b = g4.next()
                gemm1(views[0], C.KB, gmlp_j, [r_gmlp_j, wres], ci, b)
                prog.op("dve", lambda v, ci=ci, b=b: v.tensor_tensor(out=sgb[:, ci, :], in0=sgb[:, ci, :],
                                                                     in1=psum[b][:, :], op=ALU.mult),
                        reads=[pres[b], r_sgb[ci]], writes=[r_sgb[ci]])
                prog.op("dve", lambda v, ci=ci, cg=cg: v.tensor_tensor(out=merged[:, cg * 4 + ci, :], in0=sga[:, ci, :],
                                                                       in1=sgb[:, ci, :], op=ALU.add),
                        reads=[r_sga[ci], r_sgb[ci]], writes=[r_merged[cg * 4 + ci]])
        if C.stop_after == "S2":
            return W
        for cg in range((KD + 3) // 4):
            nch = min(4, KD - cg * 4)
            views, wres = W.get((w_out_d, 0, KD, [(cg * 512, nch * 128)]))
            for ci in range(nch):
                c = cg * 4 + ci
                b = g4.next()

                def fn(pe, wv=views[0], ci=ci, b=b):
                    last = None
                    for k in range(KD):
                        last = pe.matmul(psum[b][:, :], wv[:, k, ci * 128:(ci + 1) * 128], merged[:, k, :],
                                         start=(k == 0), stop=(k == KD - 1))
                    return last
                prog.op("pe", fn, reads=r_merged + [wres], writes=[pres[b]])
                prog.op("dve", lambda v, c=c: v.tensor_scalar(out=x32[:, c, :], in0=x32[:, c, :], scalar1=alpha, scalar2=None,
                                                              op0=ALU.mult), reads=[r_x32[c]], writes=[r_x32[c]])
                prog.op("dve", lambda v, c=c, b=b: v.tensor_tensor(out=x32[:, c, :], in0=x32[:, c, :], in1=psum[b][:, :],
                                                                   op=ALU.add),
                        reads=[pres[b], r_x32[c]], writes=[r_x32[c]])
                ln_stats_chunk(c, c % 2)
        ln_finish_stats()
        for c in range(KD):
            ln_apply_chunk(c, c % 2, C.V_G1, C.V_B1)
            copy_op("act", xT2[:, c, :], x32[:, c, :], [r_x32[c]], [r_xT2[c]])
        if C.stop_after == "S3":
            return W
        prog.barrier()
        for fg in range((FC + 3) // 4):
            nch = min(4, FC - fg * 4)
            views, wres = W.get((w_ff1_d, 0, KD, [(fg * 512, nch * 128)]))
            for ci in range(nch):
                f = fg * 4 + ci
                b = g4.next()
                i2 = f % 2

                def fn(pe, wv=views[0], ci=ci, b=b):
                    last = None
                    for k in range(KD):
                        last = pe.matmul(psum[b][:, :], wv[:, k, ci * 128:(ci + 1) * 128], xT2[:, k, :],
                                         start=(k == 0), stop=(k == KD - 1))
                    return last
                prog.op("pe", fn, reads=r_xT2 + [wres], writes=[pres[b]])
                prog.op("act", lambda a, b=b, i2=i2, f=f: a.activation(out=rl[i2][:, :], in_=psum[b][:, :], func=AF.Relu,
                                                                       bias=vcol(C.V_BF1 + f), scale=1.0),
                        reads=[pres[b], r_const], writes=[r_rl[i2]])
                prog.op("pool", lambda g, i2=i2, f=f: g.tensor_tensor(out=aT[:, f, :], in0=rl[i2][:, :], in1=rl[i2][:, :],
                                                                      op=ALU.mult),
                        reads=[r_rl[i2]], writes=[r_aT[f]])
        if C.stop_after == "S4":
            return W
        FG = min(16, FC)
        nfu = (FC + FG - 1) // FG
        for cb in range((KD + 3) // 4):
            nch = min(4, KD - cb * 4)
            for fu in range(nfu):
                kc = min(FG, FC - fu * FG)
                views, wres = W.get((w_ff2_d, fu * FG * 128, kc, [(cb * 512, nch * 128)]))

                def fn(pe, wv=views[0], kc=kc, fu=fu, nch=nch):
                    last = None
                    for k in range(kc):
                        f = fu * FG + k
                        for ci in range(nch):
                            last = pe.matmul(psum[ci][:, :], wv[:, k, ci * 128:(ci + 1) * 128], aT[:, f, :],
                                             start=(f == 0), stop=(f == FC - 1))
                    return last
                prog.op("pe", fn, reads=[r_aT[fu * FG + k] for k in range(kc)] + [wres],
                        writes=[pres[ci] for ci in range(nch)])
            for ci in range(nch):
                c = cb * 4 + ci
                i2 = c % 2
                prog.op("act", lambda a, ci=ci, c=c, i2=i2: a.activation(out=rl[i2][:, :], in_=psum[ci][:, :], func=AF.Identity,
                                                                         bias=vcol(C.V_BF2 + c), scale=1.0),
                        reads=[pres[ci], r_const], writes=[r_rl[i2]])
                prog.op("dve", lambda v, c=c: v.tensor_scalar(out=x32[:, c, :], in0=x32[:, c, :], scalar1=alpha, scalar2=None,
                                                              op0=ALU.mult), reads=[r_x32[c]], writes=[r_x32[c]])
                prog.op("dve", lambda v, c=c, i2=i2: v.tensor_tensor(out=x32[:, c, :], in0=x32[:, c, :], in1=rl[i2][:, :],
                                                                     op=ALU.add),
                        reads=[r_rl[i2], r_x32[c]], writes=[r_x32[c]])
                ln_stats_chunk(c, i2)
        ln_finish_stats()
        if C.stop_after == "S5":
            return W
        prog.barrier()
        for c in range(KD):
            i2 = c % 2
            ln_apply_chunk(c, i2, C.V_G2, C.V_B2)
            prog.op("act", lambda a, c=c: a.activation(out=pieces[0][:, c, :], in_=x32[:, c, :], func=AF.Copy),
                    reads=[r_x32[c]], writes=[r_piece[c]])
            prog.op("dve", lambda v, c=c, i2=i2: v.tensor_tensor(out=rr[i2][:, :], in0=x32[:, c, :], in1=pieces[0][:, c, :],
                                                                 op=ALU.subtract), reads=[r_x32[c], r_piece[c]], writes=[r_rr[i2]])
            prog.op("act", lambda a, c=c, i2=i2: a.activation(out=pieces[1][:, c, :], in_=rr[i2][:, :], func=AF.Copy),
                    reads=[r_rr[i2]], writes=[r_piece[c]])
            prog.op("dve", lambda v, c=c, i2=i2: v.tensor_tensor(out=rr[i2][:, :], in0=rr[i2][:, :], in1=pieces[1][:, c, :],
                                                                 op=ALU.subtract), reads=[r_rr[i2], r_piece[c]], writes=[r_rr[i2]])
            prog.op("act", lambda a, c=c, i2=i2: a.activation(out=pieces[2][:, c, :], in_=rr[i2][:, :], func=AF.Copy),
                    reads=[r_rr[i2]], writes=[r_piece[c]])
        for t4 in range(4):
            for cg in range((KD + 3) // 4):
                nch = min(4, KD - cg * 4)
                b = g4.next()

                def fn(pe, b=b, t4=t4, cg=cg, nch=nch):
                    last = None
                    for ci in range(nch):
                        c = cg * 4 + ci
                        for pi in range(3):
                            last = pe.matmul(psum[b][:, ci * 128:(ci + 1) * 128], pieces[pi][:, c, t4 * 128:(t4 + 1) * 128],
                                             ident[:, :], start=(pi == 0), stop=(pi == 2))
                    return last
                prog.op("pe", fn, reads=[r_piece[cg * 4 + ci] for ci in range(nch)] + [r_ident], writes=[pres[b]])
                copy_op(evac_rot.next(), otok[:, cg * 512:cg * 512 + nch * 128], psum[b][:, 0:nch * 128],
                        [pres[b]], [r_otok])
            prog.dma("sp", lambda s, t4=t4, t0=t0: s.dma_start(out=out_d[t0 + t4 * 128:t0 + (t4 + 1) * 128, :], in_=otok[:, :]),
                     d_out, reads=[r_otok])
    prog.final_wait("sp", [d_out])
    return W


def build_nc(cfg):
    from contextlib import ExitStack
    nc0 = bass.Bass("TRN2", target_bir_lowering=False)
    w0 = build_program(nc0, cfg, Prog(), None)
    plan = w0.rec
    nc = bass.Bass("TRN2", target_bir_lowering=False)
    prog = Prog()
    build_program(nc, cfg, prog, plan)
    stack = ExitStack()
    with stack:
        prog.emit(nc, stack)
    return nc


def _t5_bucket(n):
    n = np.asarray(n, dtype=np.int32)
    max_exact = N_BUCKETS // 2
    nf = np.maximum(n, 1).astype(np.float32)
    large = max_exact + (np.log(nf / np.float32(max_exact)) / np.float32(np.log(MAX_DISTANCE / max_exact))
                         * np.float32(N_BUCKETS - max_exact)).astype(np.int32)
    large = np.minimum(large, N_BUCKETS - 1)
    return np.where(n < max_exact, n, large)


def _bias_index():
    k = np.arange(128)[:, None]
    q = np.arange(128)[None, :]
    idx = np.zeros((3, 128, 256), dtype=np.int64)
    for p, (_, dil) in enumerate(PATTERNS):
        cur = np.clip(q - k, 0, 128)
        prev = np.clip(128 + q - k, 0, 128)
        idx[p, :, 0:128] = _t5_bucket(cur * dil)
        idx[p, :, 128:256] = _t5_bucket(prev * dil)
    return idx


def prep_inputs(cfg, x, w_in, rel_bias, ln_v_gain, ln_v_bias, w_spatial, b_spatial, w_proj_a, w_proj_b, w_out,
                ln1_gain, ln1_bias, w_ff1, b_ff1, w_ff2, b_ff2, ln2_gain, ln2_bias):
    C = cfg
    f = lambda a: np.ascontiguousarray(np.asarray(a, dtype=np.float32))

    def pc(v, n):
        return f(v).reshape(n, 128).T

    vecs = np.concatenate([pc(ln1_gain[0], C.KD), pc(ln1_bias[0], C.KD), pc(b_ff2[0], C.KD), pc(ln2_gain[0], C.KD),
                           pc(ln2_bias[0], C.KD), pc(b_ff1[0], C.FC)], axis=1)
    lnv = np.stack([np.broadcast_to(f(ln_v_gain[0])[None, :], (128, C.DB)),
                    np.broadcast_to(f(ln_v_bias[0])[None, :], (128, C.DB))])
    bsp = np.broadcast_to(f(b_spatial[0]).reshape(1, C.G * 128), (128, C.G * 128))
    idx = _bias_index()
    rb = f(rel_bias)
    bt = rb[idx]
    bt = np.transpose(bt, (1, 0, 3, 2)).reshape(128, 3 * C.H * 256)
    k = np.arange(128)[:, None]
    q = np.arange(128)[None, :]
    mask = np.concatenate([(q >= k), (k >= q)], axis=1).astype(np.float32)
    ident = np.eye(128, dtype=np.float32)
    shared = {
        "w_in": f(w_in[0]), "w_proj_a": f(w_proj_a[0]), "w_proj_b": f(w_proj_b[0]), "w_out": f(w_out[0]),
        "w_ff1": f(w_ff1[0]), "w_ff2": f(w_ff2[0]), "w_sp": f(w_spatial[0]), "vecs": f(vecs), "lnv": f(lnv),
        "bsp": f(bsp), "btab": f(bt), "mask": mask, "ident": ident,
    }
    xs = f(x)
    return [dict(shared, x=np.ascontiguousarray(xs[b])) for b in range(xs.shape[0])]


_NC_CACHE = {}


def kernel(**inputs):
    cfg = Cfg()
    in_maps = prep_inputs(cfg, **inputs)
    if "nc" not in _NC_CACHE:
        _NC_CACHE["nc"] = build_nc(cfg)
    nc = _NC_CACHE["nc"]
    res = run_bass_kernel_spmd(nc, in_maps, core_ids=list(range(len(in_maps))))
    out = np.stack([np.asarray(r["out"], dtype=np.float32) for r in res.results], axis=0)
    return out
```

```python
import numpy as np
import concourse.bass as bass
import concourse.mybir as mybir
from concourse.bass_utils import run_bass_kernel_spmd

F32, BF16 = mybir.dt.float32, mybir.dt.bfloat16
AF = mybir.ActivationFunctionType
ALU = mybir.AluOpType

LN_EPS = 1e-5
PATTERNS = ((128, 1), (512, 4), (2048, 16))
N_BUCKETS = 32
MAX_DISTANCE = 2048
GELU_C = 1.5957691216057308
GELU_A = 0.044715


class Cfg:
    def __init__(self, D=2048, H=8, G=8, F=8192, T=2048, depth=1, stop_after=None):
        self.stop_after = stop_after
        self.D, self.H, self.G, self.F, self.T = D, H, G, F, T
        self.KD = D // 128
        self.DA = H * 128
        self.DB = G * 128
        self.KA, self.KB = H, G
        self.FC = F // 128
        self.DIN = 3 * self.DA + 2 * self.DB + 2 * D
        self.OQ, self.OK, self.OV = 0, self.DA, 2 * self.DA
        self.OU = 3 * self.DA
        self.OVB = self.OU + self.DB
        self.OGA = self.OVB + self.DB
        self.OGB = self.OGA + D
        self.TB = 512
        self.NB = T // self.TB
        self.NT = T // 128
        self.alpha = float((2 * depth) ** 0.25)
        self.V_G1, self.V_B1, self.V_BF2, self.V_G2, self.V_B2 = [i * self.KD for i in range(5)]
        self.V_BF1 = 5 * self.KD
        self.NV = 5 * self.KD + self.FC


class Res:
    __slots__ = ("w", "r", "excl")

    def __init__(self, excl=False):
        self.w = None
        self.r = {}
        self.excl = excl


class Eng:
    def __init__(self, name):
        self.name = name
        self.count = 0
        self.ops = []
        self.waited = {}
        self.sem = None


class DSem:
    def __init__(self, name):
        self.name = name
        self.count = 0
        self.sem = None


class Prog:
    def __init__(self):
        self.eng = {n: Eng(n) for n in ("pe", "act", "dve", "pool", "sp")}
        self.dsems = []

    def dsem(self, name):
        d = DSem(name)
        self.dsems.append(d)
        return d

    def _deps(self, e, reads, writes, dma_sem=None):
        need = {}

        def add(tok):
            if tok is None:
                return
            k, v = tok
            if e.name == "pe" and k is e:
                return
            if dma_sem is not None and k is dma_sem:
                return
            if e.waited.get(k, 0) >= v:
                return
            if need.get(k, 0) < v:
                need[k] = v

        for r in reads:
            add(r.w)
            if r.excl:
                for k, v in r.r.items():
                    if k is not e:
                        add((k, v))
        for w in writes:
            add(w.w)
            for k, v in w.r.items():
                add((k, v))
        for k, v in need.items():
            e.waited[k] = v
        return list(need.items())

    def op(self, en, fn, reads=(), writes=()):
        e = self.eng[en]
        waits = self._deps(e, reads, writes)
        e.count += 1
        for r in reads:
            r.r[e] = e.count
        for w in writes:
            w.w = (e, e.count)
            w.r = {}
        e.ops.append((waits, fn, (e, 1)))

    def dma(self, en, fn, dsem, reads=(), writes=(), extra=()):
        e = self.eng[en]
        waits = self._deps(e, reads, writes, dma_sem=dsem)
        for k, v in extra:
            if e.waited.get(k, 0) < v:
                waits.append((k, v))
                e.waited[k] = v
        kind = "sw" if en == "pool" else "hw"
        assert getattr(dsem, "kind", kind) == kind, dsem.name
        dsem.kind = kind
        dsem.count += 16
        for r in reads:
            r.r[dsem] = dsem.count
        for w in writes:
            w.w = (dsem, dsem.count)
            w.r = {}
        e.ops.append((waits, fn, (dsem, 16)))

    def barrier(self):
        toks = [(e, e.count) for e in self.eng.values() if e.count > 0]
        toks += [(d, d.count) for d in self.dsems if d.count > 0]
        for e in self.eng.values():
            waits = []
            for k, v in toks:
                if k is e:
                    continue
                if e.waited.get(k, 0) < v:
                    waits.append((k, v))
                    e.waited[k] = v
            if waits:
                e.ops.append((waits, None, None))

    def final_wait(self, en, dsems):
        e = self.eng[en]
        e.ops.append(([(d, d.count) for d in dsems if d.count > 0], None, None))

    def emit(self, nc, stack):
        for e in self.eng.values():
            e.sem = stack.enter_context(nc.semaphore("s_" + e.name))
        for d in self.dsems:
            d.sem = stack.enter_context(nc.semaphore("d_" + d.name))
        block = stack.enter_context(nc.Block())

        def mk(e):
            def body(engine):
                for waits, fn, inc in e.ops:
                    for k, v in waits:
                        engine.wait_ge(k.sem, v)
                    if fn is not None:
                        inst = fn(engine)
                        inst.then_inc(inc[0].sem, inc[1])
            return body

        block.tensor(mk(self.eng["pe"]))
        block.scalar(mk(self.eng["act"]))
        block.vector(mk(self.eng["dve"]))
        block.gpsimd(mk(self.eng["pool"]))
        block.sync(mk(self.eng["sp"]))


class Rot:
    def __init__(self, items):
        self.items = list(items)
        self.i = 0

    def next(self):
        x = self.items[self.i % len(self.items)]
        self.i += 1
        return x


class WStream:
    SLOT_ELEMS = 8192

    def __init__(self, prog, nc, nslots, base_off, plan):
        self.prog = prog
        self.nslots = nslots
        self.slots = [nc.alloc_sbuf_tensor_at("wslot%d" % i, [128, self.SLOT_ELEMS], BF16,
                                              offset=base_off + i * self.SLOT_ELEMS * 2)
                      for i in range(nslots)]
        self.res = [Res() for _ in range(nslots)]
        self.dsem = [prog.dsem("w%d" % i) for i in range(nslots)]
        self.plan = plan
        self.rec = []
        self.idx = 0
        self.loaded = 0

    def _views(self, slot, spec):
        _, _, KC, segs = spec
        out = []
        off = 0
        for (_, width) in segs:
            v = self.slots[slot][:, off:off + KC * width].rearrange("p (k w) -> p k w", k=KC)
            out.append(v)
            off += KC * width
        assert off <= self.SLOT_ELEMS
        return out

    def _load(self, i, spec):
        slot = i % self.nslots
        dram, row0, KC, segs = spec
        views = self._views(slot, spec)
        hist = getattr(self, "tok_hist", [])
        ndma = sum((KC + 3) // 4 for _ in segs)
        extra = []
        if len(hist) >= 1 and (ndma > 4 or hist[-1][1] > 4):
            extra = [hist[-1][0]]
        elif len(hist) >= 2:
            extra = [hist[-2][0]]
        for (col0, width), v in zip(segs, views):
            src = dram[row0:row0 + KC * 128, col0:col0 + width].rearrange("(k p) w -> p k w", p=128)
            step = 4
            for k0 in range(0, KC, step):
                k1 = min(KC, k0 + step)
                self.prog.dma("pool",
                              (lambda g, o=v[:, k0:k1, :], s=src[:, k0:k1, :]: g.dma_start(out=o, in_=s)),
                              self.dsem[slot], writes=[self.res[slot]], extra=extra)
                extra = []
        hist.append(((self.dsem[slot], self.dsem[slot].count), ndma))
        self.tok_hist = hist[-2:]

    def get(self, spec, held=0):
        i = self.idx
        self.idx += 1
        if self.plan is None:
            self.rec.append(spec)
            target = i
        else:
            target = min(i + self.nslots - 1 - held, len(self.plan) - 1)
        while self.loaded <= target:
            self._load(self.loaded, self.plan[self.loaded] if self.plan is not None else spec)
            self.loaded += 1
        slot = i % self.nslots
        return self._views(slot, spec), self.res[slot]


def build_program(nc, cfg, prog, plan):
    C = cfg
    D, H, G, T, KD, FC, TB = C.D, C.H, C.G, C.T, C.KD, C.FC, C.TB
    NTB = T // 512

    def din(name, shape):
        return nc.dram_tensor(name, list(shape), F32, kind="ExternalInput").ap()

    x_d = din("x", [T, D])
    w_in_d = din("w_in", [D, C.DIN])
    w_pa_d = din("w_proj_a", [C.DA, D])
    w_pb_d = din("w_proj_b", [C.DB, D])
    w_out_d = din("w_out", [D, D])
    w_ff1_d = din("w_ff1", [D, C.F])
    w_ff2_d = din("w_ff2", [C.F, D])
    w_sp_d = din("w_sp", [G, 128, 128])
    vecs_d = din("vecs", [128, C.NV])
    lnv_d = din("lnv", [2, 128, C.DB])
    bsp_d = din("bsp", [128, G * 128])
    btab_d = din("btab", [128, 3 * H * 256])
    mask_d = din("mask", [128, 256])
    ident_d = din("ident", [128, 128])
    out_d = nc.dram_tensor("out", [T, D], F32, kind="ExternalOutput").ap()
    attn_s = nc.dram_tensor("attn_s", [H, 128, T], BF16).ap()
    gmlp_s = nc.dram_tensor("gmlp_s", [G, 128, T], BF16).ap()
    attn_s_res = [Res() for _ in range(H)]
    gmlp_s_res = [Res() for _ in range(G)]

    SB_BASE = 16640
    LIMIT = 224 * 1024

    class Bump:
        def __init__(self, base):
            self.off = base

        def alloc(self, name, shape, dt):
            nbytes = int(np.prod(shape[1:])) * (2 if dt == BF16 else 4)
            nbytes = (nbytes + 31) // 32 * 32
            t = nc.alloc_sbuf_tensor_at(name, list(shape), dt, offset=self.off)
            self.off += nbytes
            assert self.off <= LIMIT, (name, self.off)
            return t

    pb = Bump(SB_BASE)
    ident = pb.alloc("ident", [128, 128], BF16)
    ones = pb.alloc("ones", [128, 128], BF16)
    mask = pb.alloc("mask", [128, 256], F32)
    vecs = pb.alloc("vecs", [128, C.NV], F32)
    r_const = Res()
    r_ident = Res()
    W = WStream(prog, nc, 3, pb.off, plan)
    pb.off += 3 * WStream.SLOT_ELEMS * 2
    P_BASE = pb.off

    psum = [nc.alloc_psum_tensor("ps%d" % i, [128, 512], F32) for i in range(8)]
    pres = [Res(excl=True) for _ in range(8)]

    d_const = prog.dsem("const")
    d_ident = prog.dsem("ident")
    prog.dma("pool", lambda g: g.dma_start(out=ident[:, :], in_=ident_d[:, :]), d_ident, writes=[r_ident])
    prog.dma("sp", lambda g: g.dma_start(out=mask[:, :], in_=mask_d[:, :]), d_const, writes=[r_const])
    prog.dma("sp", lambda g: g.dma_start(out=vecs[:, :], in_=vecs_d[:, :]), d_const, writes=[r_const])
    r_ones = Res()
    prog.op("pool", lambda g: g.memset(ones[:, :], 1.0), writes=[r_ones])

    def vcol(i):
        return vecs[:, i:i + 1]

    evac_rot = Rot(["act", "dve"])

    def copy_op(en, out, in_, reads, writes):
        if en == "act":
            prog.op("act", lambda a: a.activation(out=out, in_=in_, func=AF.Copy), reads=reads, writes=writes)
        elif en == "dve":
            prog.op("dve", lambda v: v.tensor_copy(out=out, in_=in_), reads=reads, writes=writes)
        else:
            prog.op("pool", lambda g: g.tensor_copy(out=out, in_=in_), reads=reads, writes=writes)

    p1 = Bump(P_BASE)
    xT = p1.alloc("xT", [128, KD, T], BF16)
    r_xT = [Res() for _ in range(NTB)]
    R_BASE = p1.off

    rb = Bump(R_BASE)
    xb = rb.alloc("xb", [128, 4, D], BF16)
    r_xb = Res()
    d_xb = prog.dsem("xb")
    aux = Rot([4, 5, 6, 7])
    for tg in range(NTB):
        for t4 in range(4):
            t = tg * 4 + t4
            prog.dma("pool", lambda g, t=t, t4=t4: g.dma_start(out=xb[:, t4, :], in_=x_d[t * 128:(t + 1) * 128, :]),
                     d_xb, writes=[r_xb])
        en = "act" if tg % 2 == 0 else "dve"
        for c in range(KD):
            b = aux.next()

            def fn(pe, b=b, c=c):
                last = None
                for t4 in range(4):
                    last = pe.matmul(psum[b][:, t4 * 128:(t4 + 1) * 128], xb[:, t4, c * 128:(c + 1) * 128],
                                     ident[:, :], start=True, stop=True)
                return last
            prog.op("pe", fn, reads=[r_xb, r_ident], writes=[pres[b]])
            copy_op(en, xT[:, c, tg * 512:(tg + 1) * 512], psum[b][:, :], [pres[b]], [r_xT[tg]])

    if C.stop_after == "xT":
        return W
    prog.barrier()
    rb = Bump(R_BASE)
    vbn = rb.alloc("vbn", [128, C.NT, C.DB], BF16)
    r_vbn = [Res() for _ in range(C.NT)]
    uT = rb.alloc("uT", [128, 4, T], BF16)
    r_uT = Res()
    lnv = rb.alloc("lnv", [128, 2, C.DB], F32)
    bsp = rb.alloc("bspt", [128, G, 128], F32)
    wsT = rb.alloc("wsT", [128, G, 128], BF16)
    wsl = rb.alloc("wsl", [128, G, 128], BF16)
    r_tab = Res()
    r_wsl = Res()
    r_wsT = Res()
    gx2 = [rb.alloc("gx2_%d" % i, [128, 512], F32) for i in range(2)]
    gt = [rb.alloc("gt_%d" % i, [128, 512], F32) for i in range(2)]
    r_gx2 = [Res(), Res()]
    r_gt = [Res(), Res()]
    vbg = [rb.alloc("vbg_%d" % i, [128, C.DB], F32) for i in range(2)]
    r_vbg = [Res(), Res()]
    gm = rb.alloc("gm", [128, T], BF16)
    r_gm = Res()
    stats = [rb.alloc("bnst_%d" % i, [128, 4, 6], F32) for i in range(2)]
    mv = [rb.alloc("bnmv_%d" % i, [128, 4], F32) for i in range(2)]
    r_mv = [Res(), Res()]

    d_tab = prog.dsem("tab")
    prog.dma("sp", lambda g: g.dma_start(out=lnv[:, 0, :], in_=lnv_d[0, :, :]), d_tab, writes=[r_tab])
    prog.dma("sp", lambda g: g.dma_start(out=lnv[:, 1, :], in_=lnv_d[1, :, :]), d_tab, writes=[r_tab])
    prog.dma("sp", lambda g: g.dma_start(out=bsp[:, :, :], in_=bsp_d.rearrange("p (g i) -> p g i", g=G)),
             d_tab, writes=[r_tab])
    d_wsl = prog.dsem("wsl")
    for g_ in range(G):
        prog.dma("pool", lambda g, g_=g_: g.dma_start(out=wsl[:, g_, :], in_=w_sp_d[g_, :, :]), d_wsl, writes=[r_wsl])
    for g0 in range(0, G, 4):
        b = aux.next()
        ng = min(4, G - g0)

        def fn(pe, b=b, g0=g0, ng=ng):
            last = None
            for gi in range(ng):
                last = pe.matmul(psum[b][:, gi * 128:(gi + 1) * 128], wsl[:, g0 + gi, :], ident[:, :],
                                 start=True, stop=True)
            return last
        prog.op("pe", fn, reads=[r_wsl, r_ident], writes=[pres[b]])
        for gi in range(ng):
            prog.op("dve", lambda v, b=b, gi=gi, g0=g0: v.tensor_tensor(
                out=wsT[:, g0 + gi, :], in0=psum[b][:, gi * 128:(gi + 1) * 128], in1=mask[:, 0:128], op=ALU.mult),
                reads=[pres[b], r_const], writes=[r_wsT])

    gel_i = [0]

    def gelu_from_psum(b, out_ap, out_res, width=512):
        i = gel_i[0] % 2
        gel_i[0] += 1
        pv = psum[b][:, 0:width]
        x2, tt = gx2[i][:, 0:width], gt[i][:, 0:width]
        prog.op("act", lambda a: a.activation(out=x2, in_=pv, func=AF.Square), reads=[pres[b]], writes=[r_gx2[i]])
        prog.op("dve", lambda v: v.tensor_scalar(out=tt, in0=x2, scalar1=GELU_C * GELU_A, scalar2=GELU_C,
                                                 op0=ALU.mult, op1=ALU.add), reads=[r_gx2[i]], writes=[r_gt[i]])
        prog.op("dve", lambda v: v.tensor_tensor(out=tt, in0=tt, in1=pv, op=ALU.mult),
                reads=[r_gt[i], pres[b]], writes=[r_gt[i]])
        prog.op("act", lambda a: a.activation(out=x2, in_=tt, func=AF.Sigmoid), reads=[r_gt[i]], writes=[r_gx2[i]])
        prog.op("dve", lambda v: v.tensor_tensor(out=out_ap, in0=x2, in1=pv, op=ALU.mult),
                reads=[r_gx2[i], pres[b]], writes=[out_res])

    nvu = (C.DB + 511) // 512
    vb_units = []
    for vg in range(nvu):
        wdt = min(512, C.DB - vg * 512)
        views, wres = W.get((w_in_d, 0, KD, [(C.OVB + vg * 512, wdt)]), held=vg)
        vb_units.append((views[0], wres, wdt))
    gb = Rot([0, 1, 2, 3])
    for t in range(C.NT):
        i2 = t % 2
        tg = t // 4
        for vg, (wv, wres, wdt) in enumerate(vb_units):
            b = gb.next()

            def fn(pe, b=b, t=t, wv=wv, wdt=wdt):
                last = None
                for k in range(KD):
                    last = pe.matmul(psum[b][:, 0:wdt], xT[:, k, t * 128:(t + 1) * 128], wv[:, k, :],
                                     start=(k == 0), stop=(k == KD - 1))
                return last
            prog.op("pe", fn, reads=[r_xT[tg], wres], writes=[pres[b]])
            gelu_from_psum(b, vbg[i2][:, vg * 512:vg * 512 + wdt], r_vbg[i2], wdt)
        nch = (C.DB + 511) // 512

        def fst(v, i2=i2, nch=nch):
            last = None
            for ch in range(nch):
                w_ = min(512, C.DB - ch * 512)
                last = v.bn_stats(out=stats[i2][:, ch, :], in_=vbg[i2][:, ch * 512:ch * 512 + w_])
            return last
        prog.op("dve", fst, reads=[r_vbg[i2]], writes=[r_mv[i2]])
        prog.op("dve", lambda v, i2=i2, nch=nch: v.bn_aggr(out=mv[i2][:, 0:2], in_=stats[i2][:, 0:nch, :]),
                reads=[r_mv[i2]], writes=[r_mv[i2]])
        prog.op("act", lambda a, i2=i2: a.activation(out=mv[i2][:, 2:3], in_=mv[i2][:, 1:2], func=AF.Sqrt,
                                                     bias=float(LN_EPS), scale=1.0),
                reads=[r_mv[i2]], writes=[r_mv[i2]])
        prog.op("dve", lambda v, i2=i2: v.reciprocal(out=mv[i2][:, 3:4], in_=mv[i2][:, 2:3]),
                reads=[r_mv[i2]], writes=[r_mv[i2]])
        prog.op("dve", lambda v, i2=i2: v.tensor_scalar(out=vbg[i2][:, :], in0=vbg[i2][:, :], scalar1=mv[i2][:, 0:1],
                                                        scalar2=mv[i2][:, 3:4], op0=ALU.subtract, op1=ALU.mult),
                reads=[r_mv[i2], r_vbg[i2]], writes=[r_vbg[i2]])
        prog.op("dve", lambda v, i2=i2: v.tensor_tensor(out=vbg[i2][:, :], in0=vbg[i2][:, :], in1=lnv[:, 0, :],
                                                        op=ALU.mult), reads=[r_vbg[i2], r_tab], writes=[r_vbg[i2]])
        prog.op("dve", lambda v, i2=i2, t=t: v.tensor_tensor(out=vbn[:, t, :], in0=vbg[i2][:, :], in1=lnv[:, 1, :],
                                                             op=ALU.add), reads=[r_vbg[i2], r_tab], writes=[r_vbn[t]])

    d_gm = prog.dsem("gm")
    for ug in range((G + 3) // 4):
        ng = min(4, G - ug * 4)
        views, wres = W.get((w_in_d, 0, KD, [(C.OU + ug * 512, ng * 128)]))
        wu = views[0]
        for gi in range(ng):
            g_ = ug * 4 + gi

            def fn(pe, wu=wu, gi=gi):
                last = None
                for k in range(KD):
                    for tb in range(NTB):
                        last = pe.matmul(psum[tb][:, :], wu[:, k, gi * 128:(gi + 1) * 128],
                                         xT[:, k, tb * 512:(tb + 1) * 512], start=(k == 0), stop=(k == KD - 1))
                return last
            prog.op("pe", fn, reads=r_xT + [wres], writes=[pres[tb] for tb in range(NTB)])
            for tb in range(NTB):
                gelu_from_psum(tb, uT[:, gi, tb * 512:(tb + 1) * 512], r_uT)
            for tb in range(NTB):
                b = aux.next()

                def fz(pe, b=b, tb=tb, g_=g_):
                    last = None
                    for c4 in range(4):
                        t = tb * 4 + c4
                        last = pe.matmul(psum[b][:, c4 * 128:(c4 + 1) * 128], vbn[:, t, g_ * 128:(g_ + 1) * 128],
                                         wsT[:, g_, :], start=True, stop=True)
                    return last
                prog.op("pe", fz, reads=[r_vbn[tb * 4 + c4] for c4 in range(4)] + [r_wsT], writes=[pres[b]])
                i = gel_i[0] % 2
                gel_i[0] += 1
                zt = gt[i]
                prog.op("dve", lambda v, b=b, zt=zt, g_=g_: v.tensor_tensor(
                    out=zt[:, :].rearrange("p (c i) -> p c i", c=4),
                    in0=psum[b][:, :].rearrange("p (c i) -> p c i", c=4),
                    in1=bsp[:, g_:g_ + 1, :].to_broadcast([128, 4, 128]), op=ALU.add),
                    reads=[pres[b], r_tab], writes=[r_gt[i]])
                prog.op("dve", lambda v, zt=zt, gi=gi, tb=tb: v.tensor_tensor(
                    out=gm[:, tb * 512:(tb + 1) * 512], in0=zt[:, :], in1=uT[:, gi, tb * 512:(tb + 1) * 512],
                    op=ALU.mult), reads=[r_gt[i], r_uT], writes=[r_gm])
            prog.dma("sp", lambda s, g_=g_: s.dma_start(out=gmlp_s[g_, :, :], in_=gm[:, :]), d_gm,
                     reads=[r_gm], writes=[gmlp_s_res[g_]])

    if C.stop_after == "1a":
        return W
    prog.barrier()
    rb = Bump(R_BASE)
    ebt = rb.alloc("ebt", [128, 3 * H, 256], F32)
    r_ebt = Res()
    qkv = [rb.alloc("qkv%d" % i, [128, T], BF16) for i in range(3)]
    r_qkv = [Res(), Res(), Res()]
    Vp = [rb.alloc("Vp%d" % i, [128, 16, 128], BF16) for i in range(3)]
    r_Vp = [Res(), Res(), Res()]
    acc = rb.alloc("acc", [128, 2, T], F32)
    r_acc = Res()
    esb = [rb.alloc("esb%d" % i, [128, 256], F32) for i in range(4)]
    r_esb = [Res() for _ in range(4)]
    ptb = [rb.alloc("ptb%d" % i, [128, 256], BF16) for i in range(4)]
    r_ptb = [Res() for _ in range(4)]
    ast = rb.alloc("ast", [128, T], BF16)
    r_ast = Res()

    d_bt = prog.dsem("btab")
    prog.dma("sp", lambda s: s.dma_start(out=ebt[:, :, :], in_=btab_d.rearrange("p (a b) -> p a b", b=256)),
             d_bt, writes=[r_ebt])
    prog.op("act", lambda a: a.activation(out=ebt[:, :, :], in_=ebt[:, :, :], func=AF.Exp), reads=[r_ebt], writes=[r_ebt])
    prog.op("dve", lambda v: v.tensor_tensor(out=ebt[:, :, :], in0=ebt[:, :, :],
                                             in1=mask[:, :].unsqueeze(1).to_broadcast([128, 3 * H, 256]), op=ALU.mult),
            reads=[r_ebt, r_const], writes=[r_ebt])

    d_ast = prog.dsem("ast")
    scb = Rot([4, 5, 0, 1])
    ob = Rot([6, 7, 2, 3])
    scale = float(128 ** -0.5)
    u_i = 0
    for h in range(H):
        views, wres = W.get((w_in_d, 0, KD, [(C.OQ + h * 128, 128), (C.OK + h * 128, 128), (C.OV + h * 128, 128)]))
        for j in range(3):
            wv = views[j]

            def fn(pe, wv=wv):
                last = None
                for k in range(KD):
                    for tb in range(NTB):
                        last = pe.matmul(psum[tb][:, :], wv[:, k, :], xT[:, k, tb * 512:(tb + 1) * 512],
                                         start=(k == 0), stop=(k == KD - 1))
                return last
            prog.op("pe", fn, reads=r_xT + [wres], writes=[pres[tb] for tb in range(NTB)])
            for tb in range(NTB):
                copy_op("act" if tb % 2 == 0 else "dve", qkv[j][:, tb * 512:(tb + 1) * 512], psum[tb][:, :],
                        [pres[tb]], [r_qkv[j]])
        qT, kT, vT = qkv
        for p, (_, dil) in enumerate(PATTERNS):
            nblk = T // dil // 128
            for q4 in range(4):
                b = scb.next() if q4 % 2 == 0 else ob.next()

                def fv(pe, b=b, p=p, dil=dil, nblk=nblk, q4=q4):
                    last = None
                    for i4 in range(4):
                        blk = q4 * 4 + i4
                        r, n = blk // nblk, blk % nblk
                        st = r + dil * 128 * n
                        last = pe.matmul(psum[b][:, i4 * 128:(i4 + 1) * 128], vT[:, st:st + dil * 127 + 1:dil],
                                         ident[:, :], start=True, stop=True)
                    return last
                prog.op("pe", fv, reads=[r_qkv[2], r_ident], writes=[pres[b]])
                copy_op("act" if q4 % 2 == 0 else "dve",
                        Vp[p][:, q4 * 4:(q4 + 1) * 4, :], psum[b][:, :].rearrange("p (a b) -> p a b", a=4),
                        [pres[b]], [r_Vp[p]])
        for p, (_, dil) in enumerate(PATTERNS):
            nblk = T // dil // 128
            tab = ebt[:, p * H + h, :]
            for r in range(dil):
                for n in range(nblk):
                    blk = r * nblk + n
                    st = r + dil * 128 * n
                    toks = slice(st, st + dil * 127 + 1, dil)
                    hp = n > 0
                    ptoks = slice(st - dil * 128, st - dil * 128 + dil * 127 + 1, dil)
                    wd = 256 if hp else 128
                    sb_ = scb.next()
                    o_ = ob.next()
                    i2 = u_i % 4
                    u_i += 1

                    def fs(pe, sb_=sb_, toks=toks, ptoks=ptoks, hp=hp):
                        last = pe.matmul(psum[sb_][:, 0:128], kT[:, toks], qT[:, toks], start=True, stop=True)
                        if hp:
                            last = pe.matmul(psum[sb_][:, 128:256], kT[:, ptoks], qT[:, toks], start=True, stop=True)
                        return last
                    prog.op("pe", fs, reads=[r_qkv[0], r_qkv[1]], writes=[pres[sb_]])
                    prog.op("act", lambda a, sb_=sb_, i2=i2, wd=wd: a.activation(
                        out=esb[i2][:, 0:wd], in_=psum[sb_][:, 0:wd], func=AF.Exp, scale=scale),
                        reads=[pres[sb_]], writes=[r_esb[i2]])
                    prog.op("pool", lambda g, i2=i2, wd=wd, tab=tab: g.tensor_tensor(
                        out=ptb[i2][:, 0:wd], in0=esb[i2][:, 0:wd], in1=tab[:, 0:wd], op=ALU.mult),
                        reads=[r_esb[i2], r_ebt], writes=[r_ptb[i2]])

                    def fo(pe, o_=o_, i2=i2, p=p, blk=blk, hp=hp):
                        pe.matmul(psum[o_][:, 0:128], Vp[p][:, blk, :], ptb[i2][:, 0:128], start=True, stop=not hp)
                        if hp:
                            pe.matmul(psum[o_][:, 0:128], Vp[p][:, blk - 1, :], ptb[i2][:, 128:256], start=False, stop=True)
                        last = pe.matmul(psum[o_][:, 128:256], ones[:, :], ptb[i2][:, 0:128], start=True, stop=not hp)
                        if hp:
                            last = pe.matmul(psum[o_][:, 128:256], ones[:, :], ptb[i2][:, 128:256], start=False, stop=True)
                        return last
                    prog.op("pe", fo, reads=[r_ptb[i2], r_Vp[p], r_ones], writes=[pres[o_]])
                    ov = psum[o_][:, 0:256].rearrange("p (a b) -> p a b", a=2)
                    av = acc[:, :, toks]
                    if p == 0:
                        prog.op("dve", lambda v, av=av, ov=ov: v.tensor_copy(out=av, in_=ov),
                                reads=[pres[o_]], writes=[r_acc])
                    else:
                        prog.op("dve", lambda v, av=av, ov=ov: v.tensor_tensor(out=av, in0=av, in1=ov, op=ALU.add),
                                reads=[pres[o_], r_acc], writes=[r_acc])
        prog.op("dve", lambda v: v.reciprocal(out=acc[:, 1, :], in_=acc[:, 1, :]), reads=[r_acc], writes=[r_acc])
        prog.op("dve", lambda v: v.tensor_tensor(out=ast[:, :], in0=acc[:, 0, :], in1=acc[:, 1, :], op=ALU.mult),
                reads=[r_acc], writes=[r_ast])
        prog.dma("sp", lambda s, h=h: s.dma_start(out=attn_s[h, :, :], in_=ast[:, :]), d_ast,
                 reads=[r_ast], writes=[attn_s_res[h]])

    if C.stop_after == "1b":
        return W
    prog.barrier()
    p2 = Bump(P_BASE)
    x32 = p2.alloc("x32", [128, KD, TB], F32)
    r_x32 = [Res() for _ in range(KD)]
    xT2 = p2.alloc("xT2", [128, KD, TB], BF16)
    r_xT2 = [Res() for _ in range(KD)]
    ARENA = p2.off
    ARENA_SZ = max(64 * 1024, FC * TB * 2)
    p2.off += ARENA_SZ
    r_xtok, r_xhi, r_xlo, r_attn_j, r_gmlp_j, r_otok = Res(), Res(), Res(), Res(), Res(), Res()
    r_merged = [Res() for _ in range(KD)]
    r_sga = [Res() for _ in range(4)]
    r_sgb = [Res() for _ in range(4)]
    r_aT = [Res() for _ in range(FC)]
    r_piece = [Res() for _ in range(KD)]
    r_rr = [Res(), Res()]
    hb = [p2.alloc("hb%d" % i, [128, TB], BF16) for i in range(2)]
    sq = [p2.alloc("sq%d" % i, [128, TB], BF16) for i in range(2)]
    r_hb = [Res(), Res()]
    r_sq = [Res(), Res()]
    mean = p2.alloc("mean", [128, TB], F32)
    rstd = p2.alloc("rstd", [128, TB], F32)
    ex2 = p2.alloc("ex2", [128, TB], F32)
    r_st = Res()
    tt2 = [p2.alloc("tt%d" % i, [128, TB], F32) for i in range(2)]
    r_tt2 = [Res(), Res()]
    rl = [p2.alloc("rl%d" % i, [128, TB], F32) for i in range(2)]
    r_rl = [Res(), Res()]

    def arena_alloc(name, shape, dt, off):
        nbytes = int(np.prod(shape[1:])) * (2 if dt == BF16 else 4)
        assert off + nbytes <= ARENA_SZ, (name, off, nbytes)
        return nc.alloc_sbuf_tensor_at(name, list(shape), dt, offset=ARENA + off)

    xtok = arena_alloc("xtok", [128, 4, D], F32, 0)
    xhi = arena_alloc("xhi", [128, 4, D], BF16, 4 * D * 4)
    xlo = arena_alloc("xlo", [128, 4, D], BF16, 4 * D * 6)
    attn_j = arena_alloc("attn_j", [128, C.KA, TB], BF16, 0)
    gmlp_j = arena_alloc("gmlp_j", [128, C.KB, TB], BF16, C.KA * TB * 2)
    o2 = (C.KA + C.KB) * TB * 2
    merged = arena_alloc("merged", [128, KD, TB], BF16, o2)
    o2 += KD * TB * 2
    sga = arena_alloc("sga", [128, 4, TB], F32, o2)
    sgb = arena_alloc("sgb", [128, 4, TB], F32, o2 + 4 * TB * 4)
    aT = arena_alloc("aT", [128, FC, TB], BF16, 0)
    pieces = [arena_alloc("piece%d" % i, [128, KD, TB], BF16, i * KD * TB * 2) for i in range(3)]
    o6 = 3 * KD * TB * 2
    otok = arena_alloc("otok", [128, D], F32, o6)
    rr = [arena_alloc("rr%d" % i, [128, TB], F32, o6 + D * 4 + i * TB * 4) for i in range(2)]

    d_x = prog.dsem("xtok")
    d_aj = prog.dsem("aj")
    d_gj = prog.dsem("gj")
    d_out = prog.dsem("out")
    g4 = Rot([0, 1, 2, 3])
    s67 = Rot([6, 7])
    alpha = C.alpha
    invD = 1.0 / D

    def ln_stats_chunk(c, i2):
        prog.op("act", lambda a, c=c, i2=i2: a.activation(out=hb[i2][:, :], in_=x32[:, c, :], func=AF.Copy),
                reads=[r_x32[c]], writes=[r_hb[i2]])
        prog.op("act", lambda a, c=c, i2=i2: a.activation(out=sq[i2][:, :], in_=x32[:, c, :], func=AF.Square),
                reads=[r_x32[c]], writes=[r_sq[i2]])

        def fn(pe, c=c, i2=i2):
            pe.matmul(psum[4][:, :], ones[:, :], hb[i2][:, :], start=(c == 0), stop=(c == KD - 1))
            return pe.matmul(psum[5][:, :], ones[:, :], sq[i2][:, :], start=(c == 0), stop=(c == KD - 1))
        prog.op("pe", fn, reads=[r_hb[i2], r_sq[i2], r_ones], writes=[pres[4], pres[5]])

    def ln_finish_stats():
        prog.op("dve", lambda v: v.tensor_scalar(out=mean[:, :], in0=psum[4][:, :], scalar1=invD, scalar2=None,
                                                 op0=ALU.mult), reads=[pres[4]], writes=[r_st])
        prog.op("dve", lambda v: v.tensor_scalar(out=ex2[:, :], in0=psum[5][:, :], scalar1=invD, scalar2=None,
                                                 op0=ALU.mult), reads=[pres[5], r_st], writes=[r_st])
        prog.op("dve", lambda v: v.tensor_tensor(out=rstd[:, :], in0=mean[:, :], in1=mean[:, :], op=ALU.mult),
                reads=[r_st], writes=[r_st])
        prog.op("dve", lambda v: v.tensor_tensor(out=ex2[:, :], in0=ex2[:, :], in1=rstd[:, :], op=ALU.subtract),
                reads=[r_st], writes=[r_st])
        prog.op("act", lambda a: a.activation(out=ex2[:, :], in_=ex2[:, :], func=AF.Sqrt, bias=float(LN_EPS), scale=1.0),
                reads=[r_st], writes=[r_st])
        prog.op("dve", lambda v: v.reciprocal(out=rstd[:, :], in_=ex2[:, :]), reads=[r_st], writes=[r_st])

    def ln_apply_chunk(c, i2, gcol, bcol):
        prog.op("dve", lambda v, c=c, i2=i2: v.tensor_tensor(out=tt2[i2][:, :], in0=x32[:, c, :], in1=mean[:, :],
                                                             op=ALU.subtract), reads=[r_x32[c], r_st], writes=[r_tt2[i2]])
        prog.op("dve", lambda v, i2=i2: v.tensor_tensor(out=tt2[i2][:, :], in0=tt2[i2][:, :], in1=rstd[:, :],
                                                        op=ALU.mult), reads=[r_tt2[i2], r_st], writes=[r_tt2[i2]])
        prog.op("dve", lambda v, c=c, i2=i2: v.tensor_scalar(out=x32[:, c, :], in0=tt2[i2][:, :], scalar1=vcol(gcol + c),
                                                             scalar2=vcol(bcol + c), op0=ALU.mult, op1=ALU.add),
                reads=[r_tt2[i2], r_const], writes=[r_x32[c]])

    for j in range(C.NB):
        t0 = j * TB
        prog.barrier()
        for t4 in range(4):
            prog.dma("pool", lambda s, t4=t4, t0=t0: s.dma_start(out=xtok[:, t4, :],
                                                                  in_=x_d[t0 + t4 * 128:t0 + (t4 + 1) * 128, :]),
                     d_x, writes=[r_xtok])
        for t4 in range(4):
            prog.op("act", lambda a, t4=t4: a.activation(out=xhi[:, t4, :], in_=xtok[:, t4, :], func=AF.Copy),
                    reads=[r_xtok], writes=[r_xhi])
        for t4 in range(4):
            prog.op("dve", lambda v, t4=t4: v.tensor_tensor(out=xtok[:, t4, :], in0=xtok[:, t4, :], in1=xhi[:, t4, :],
                                                            op=ALU.subtract),
                    reads=[r_xtok, r_xhi], writes=[r_xtok])
        for t4 in range(4):
            prog.op("act", lambda a, t4=t4: a.activation(out=xlo[:, t4, :], in_=xtok[:, t4, :], func=AF.Copy),
                    reads=[r_xtok], writes=[r_xlo])
        for c in range(KD):
            b = s67.next()

            def fn(pe, b=b, c=c):
                last = None
                for t4 in range(4):
                    pe.matmul(psum[b][:, t4 * 128:(t4 + 1) * 128], xhi[:, t4, c * 128:(c + 1) * 128], ident[:, :],
                              start=True, stop=False)
                    last = pe.matmul(psum[b][:, t4 * 128:(t4 + 1) * 128], xlo[:, t4, c * 128:(c + 1) * 128], ident[:, :],
                                     start=False, stop=True)
                return last
            prog.op("pe", fn, reads=[r_xhi, r_xlo, r_ident], writes=[pres[b]])
            copy_op("dve", x32[:, c, :], psum[b][:, :], [pres[b]], [r_x32[c]])
            copy_op("act", xT2[:, c, :], psum[b][:, :], [pres[b]], [r_xT2[c]])
        if C.stop_after == "S1":
            return W
        prog.barrier()
        prog.dma("pool", lambda s, t0=t0: s.dma_start(out=attn_j[:, :, :],
                                                      in_=attn_s[:, :, t0:t0 + TB].rearrange("h p t -> p h t")),
                 d_aj, reads=attn_s_res, writes=[r_attn_j])
        prog.dma("pool", lambda s, t0=t0: s.dma_start(out=gmlp_j[:, :, :],
                                                      in_=gmlp_s[:, :, t0:t0 + TB].rearrange("g p t -> p g t")),
                 d_gj, reads=gmlp_s_res, writes=[r_gmlp_j])
        for cg in range((KD + 3) // 4):
            nch = min(4, KD - cg * 4)
            wdt = nch * 128

            def gemm1(wv, KC, src, src_res, ci, b):
                def fn(pe):
                    last = None
                    for k in range(KC):
                        last = pe.matmul(psum[b][:, :], wv[:, k, ci * 128:(ci + 1) * 128], src[:, k, :],
                                         start=(k == 0), stop=(k == KC - 1))
                    return last
                prog.op("pe", fn, reads=src_res, writes=[pres[b]])

            views, wres = W.get((w_in_d, 0, KD, [(C.OGA + cg * 512, wdt)]))
            for ci in range(nch):
                b = g4.next()
                gemm1(views[0], KD, xT2, r_xT2 + [wres], ci, b)
                prog.op("act", lambda a, ci=ci, b=b: a.activation(out=sga[:, ci, :], in_=psum[b][:, :], func=AF.Sigmoid),
                        reads=[pres[b]], writes=[r_sga[ci]])
            views, wres = W.get((w_pa_d, 0, C.KA, [(cg * 512, wdt)]))
            for ci in range(nch):
                b = g4.next()
                gemm1(views[0], C.KA, attn_j, [r_attn_j, wres], ci, b)
                prog.op("dve", lambda v, ci=ci, b=b: v.tensor_tensor(out=sga[:, ci, :], in0=sga[:, ci, :],
                                                                     in1=psum[b][:, :], op=ALU.mult),
                        reads=[pres[b], r_sga[ci]], writes=[r_sga[ci]])
            views, wres = W.get((w_in_d, 0, KD, [(C.OGB + cg * 512, wdt)]))
            for ci in range(nch):
                b = g4.next()
                gemm1(views[0], KD, xT2, r_xT2 + [wres], ci, b)
                prog.op("act", lambda a, ci=ci, b=b: a.activation(out=sgb[:, ci, :], in_=psum[b][:, :], func=AF.Sigmoid),
                        reads=[pres[b]], writes=[r_sgb[ci]])
            views, wres = W.get((w_pb_d, 0, C.KB, [(cg * 512, wdt)]))
            for ci in range(nch):
                b = g4.next()
                gemm1(views[0], C.KB, gmlp_j, [r_gmlp_j, wres], ci, b)
                prog.op("dve", lambda v, ci=ci, b=b: v.tensor_tensor(out=sgb[:, ci, :], in0=sgb[:, ci, :],
                                                                     in1=psum[b][:, :], op=ALU.mult),
                        reads=[pres[b], r_sgb[ci]], writes=[r_sgb[ci]])
                prog.op("dve", lambda v, ci=ci, cg=cg: v.tensor_tensor(out=merged[:, cg * 4 + ci, :], in0=sga[:, ci, :],
                                                                       in1=sgb[:, ci, :], op=ALU.add),
                        reads=[r_sga[ci], r_sgb[ci]], writes=[r_merged[cg * 4 + ci]])
        if C.stop_after == "S2":
            return W
        for cg in range((KD + 3) // 4):
            nch = min(4, KD - cg * 4)
            views, wres = W.get((w_out_d, 0, KD, [(cg * 512, nch * 128)]))
            for ci in range(nch):
                c = cg * 4 + ci
                b = g4.next()

                def fn(pe, wv=views[0], ci=ci, b=b):
                    last = None
                    for k in range(KD):
                        last = pe.matmul(psum[b][:, :], wv[:, k, ci * 128:(ci + 1) * 128], merged[:, k, :],
                                         start=(k == 0), stop=(k == KD - 1))
                    return last
                prog.op("pe", fn, reads=r_merged + [wres], writes=[pres[b]])
                prog.op("dve", lambda v, c=c: v.tensor_scalar(out=x32[:, c, :], in0=x32[:, c, :], scalar1=alpha, scalar2=None,
                                                              op0=ALU.mult), reads=[r_x32[c]], writes=[r_x32[c]])
                prog.op("dve", lambda v, c=c, b=b: v.tensor_tensor(out=x32[:, c, :], in0=x32[:, c, :], in1=psum[b][:, :],
                                                                   op=ALU.add),
                        reads=[pres[b], r_x32[c]], writes=[r_x32[c]])
                ln_stats_chunk(c, c % 2)
        ln_finish_stats()
        for c in range(KD):
            ln_apply_chunk(c, c % 2, C.V_G1, C.V_B1)
            copy_op("act", xT2[:, c, :], x32[:, c, :], [r_x32[c]], [r_xT2[c]])
        if C.stop_after == "S3":
            return W
        prog.barrier()
        for fg in range((FC + 3) // 4):
            nch = min(4, FC - fg * 4)
            views, wres = W.get((w_ff1_d, 0, KD, [(fg * 512, nch * 128)]))
            for ci in range(nch):
                f = fg * 4 + ci
                b = g4.next()
                i2 = f % 2

                def fn(pe, wv=views[0], ci=ci, b=b):
                    last = None
                    for k in range(KD):
                        last = pe.matmul(psum[b][:, :], wv[:, k, ci * 128:(ci + 1) * 128], xT2[:, k, :],
                                         start=(k == 0), stop=(k == KD - 1))
                    return last
                prog.op("pe", fn, reads=r_xT2 + [wres], writes=[pres[b]])
                prog.op("act", lambda a, b=b, i2=i2, f=f: a.activation(out=rl[i2][:, :], in_=psum[b][:, :], func=AF.Relu,
                                                                       bias=vcol(C.V_BF1 + f), scale=1.0),
                        reads=[pres[b], r_const], writes=[r_rl[i2]])
                prog.op("pool", lambda g, i2=i2, f=f: g.tensor_tensor(out=aT[:, f, :], in0=rl[i2][:, :], in1=rl[i2][:, :],
                                                                      op=ALU.mult),
                        reads=[r_rl[i2]], writes=[r_aT[f]])
        if C.stop_after == "S4":
            return W
        FG = min(16, FC)
        nfu = (FC + FG - 1) // FG
        for cb in range((KD + 3) // 4):
            nch = min(4, KD - cb * 4)
            for fu in range(nfu):
                kc = min(FG, FC - fu * FG)
                views, wres = W.get((w_ff2_d, fu * FG * 128, kc, [(cb * 512, nch * 128)]))

                def fn(pe, wv=views[0], kc=kc, fu=fu, nch=nch):
                    last = None
                    for k in range(kc):
                        f = fu * FG + k
                        for ci in range(nch):
                            last = pe.matmul(psum[ci][:, :], wv[:, k, ci * 128:(ci + 1) * 128], aT[:, f, :],
                                             start=(f == 0), stop=(f == FC - 1))
                    return last
                prog.op("pe", fn, reads=[r_aT[fu * FG + k] for k in range(kc)] + [wres],
                        writes=[pres[ci] for ci in range(nch)])
            for ci in range(nch):
                c = cb * 4 + ci
                i2 = c % 2
                prog.op("act", lambda a, ci=ci, c=c, i2=i2: a.activation(out=rl[i2][:, :], in_=psum[ci][:, :], func=AF.Identity,
                                                                         bias=vcol(C.V_BF2 + c), scale=1.0),
                        reads=[pres[ci], r_const], writes=[r_rl[i2]])
                prog.op("dve", lambda v, c=c: v.tensor_scalar(out=x32[:, c, :], in0=x32[:, c, :], scalar1=alpha, scalar2=None,
                                                              op0=ALU.mult), reads=[r_x32[c]], writes=[r_x32[c]])
                prog.op("dve", lambda v, c=c, i2=i2: v.tensor_tensor(out=x32[:, c, :], in0=x32[:, c, :], in1=rl[i2][:, :],
                                                                     op=ALU.add),
                        reads=[r_rl[i2], r_x32[c]], writes=[r_x32[c]])
                ln_stats_chunk(c, i2)
        ln_finish_stats()
        if C.stop_after == "S5":
            return W
        prog.barrier()
        for c in range(KD):
            i2 = c % 2
            ln_apply_chunk(c, i2, C.V_G2, C.V_B2)
            prog.op("act", lambda a, c=c: a.activation(out=pieces[0][:, c, :], in_=x32[:, c, :], func=AF.Copy),
                    reads=[r_x32[c]], writes=[r_piece[c]])
            prog.op("dve", lambda v, c=c, i2=i2: v.tensor_tensor(out=rr[i2][:, :], in0=x32[:, c, :], in1=pieces[0][:, c, :],
                                                                 op=ALU.subtract), reads=[r_x32[c], r_piece[c]], writes=[r_rr[i2]])
            prog.op("act", lambda a, c=c, i2=i2: a.activation(out=pieces[1][:, c, :], in_=rr[i2][:, :], func=AF.Copy),
                    reads=[r_rr[i2]], writes=[r_piece[c]])
            prog.op("dve", lambda v, c=c, i2=i2: v.tensor_tensor(out=rr[i2][:, :], in0=rr[i2][:, :], in1=pieces[1][:, c, :],
                                                                 op=ALU.subtract), reads=[r_rr[i2], r_piece[c]], writes=[r_rr[i2]])
            prog.op("act", lambda a, c=c, i2=i2: a.activation(out=pieces[2][:, c, :], in_=rr[i2][:, :], func=AF.Copy),
                    reads=[r_rr[i2]], writes=[r_piece[c]])
        for t4 in range(4):
            for cg in range((KD + 3) // 4):
                nch = min(4, KD - cg * 4)
                b = g4.next()

                def fn(pe, b=b, t4=t4, cg=cg, nch=nch):
                    last = None
                    for ci in range(nch):
                        c = cg * 4 + ci
                        for pi in range(3):
                            last = pe.matmul(psum[b][:, ci * 128:(ci + 1) * 128], pieces[pi][:, c, t4 * 128:(t4 + 1) * 128],
                                             ident[:, :], start=(pi == 0), stop=(pi == 2))
                    return last
                prog.op("pe", fn, reads=[r_piece[cg * 4 + ci] for ci in range(nch)] + [r_ident], writes=[pres[b]])
                copy_op(evac_rot.next(), otok[:, cg * 512:cg * 512 + nch * 128], psum[b][:, 0:nch * 128],
                        [pres[b]], [r_otok])
            prog.dma("sp", lambda s, t4=t4, t0=t0: s.dma_start(out=out_d[t0 + t4 * 128:t0 + (t4 + 1) * 128, :], in_=otok[:, :]),
                     d_out, reads=[r_otok])
    prog.final_wait("sp", [d_out])
    return W


def build_nc(cfg):
    from contextlib import ExitStack
    nc0 = bass.Bass("TRN2", target_bir_lowering=False)
    w0 = build_program(nc0, cfg, Prog(), None)
    plan = w0.rec
    nc = bass.Bass("TRN2", target_bir_lowering=False)
    prog = Prog()
    build_program(nc, cfg, prog, plan)
    stack = ExitStack()
    with stack:
        prog.emit(nc, stack)
    return nc


def _t5_bucket(n):
    n = np.asarray(n, dtype=np.int32)
    max_exact = N_BUCKETS // 2
    nf = np.maximum(n, 1).astype(np.float32)
    large = max_exact + (np.log(nf / np.float32(max_exact)) / np.float32(np.log(MAX_DISTANCE / max_exact))
                         * np.float32(N_BUCKETS - max_exact)).astype(np.int32)
    large = np.minimum(large, N_BUCKETS - 1)
    return np.where(n < max_exact, n, large)


def _bias_index():
    k = np.arange(128)[:, None]
    q = np.arange(128)[None, :]
    idx = np.zeros((3, 128, 256), dtype=np.int64)
    for p, (_, dil) in enumerate(PATTERNS):
        cur = np.clip(q - k, 0, 128)
        prev = np.clip(128 + q - k, 0, 128)
        idx[p, :, 0:128] = _t5_bucket(cur * dil)
        idx[p, :, 128:256] = _t5_bucket(prev * dil)
    return idx


def prep_inputs(cfg, x, w_in, rel_bias, ln_v_gain, ln_v_bias, w_spatial, b_spatial, w_proj_a, w_proj_b, w_out,
                ln1_gain, ln1_bias, w_ff1, b_ff1, w_ff2, b_ff2, ln2_gain, ln2_bias):
    C = cfg
    f = lambda a: np.ascontiguousarray(np.asarray(a, dtype=np.float32))

    def pc(v, n):
        return f(v).reshape(n, 128).T

    vecs = np.concatenate([pc(ln1_gain[0], C.KD), pc(ln1_bias[0], C.KD), pc(b_ff2[0], C.KD), pc(ln2_gain[0], C.KD),
                           pc(ln2_bias[0], C.KD), pc(b_ff1[0], C.FC)], axis=1)
    lnv = np.stack([np.broadcast_to(f(ln_v_gain[0])[None, :], (128, C.DB)),
                    np.broadcast_to(f(ln_v_bias[0])[None, :], (128, C.DB))])
    bsp = np.broadcast_to(f(b_spatial[0]).reshape(1, C.G * 128), (128, C.G * 128))
    idx = _bias_index()
    rb = f(rel_bias)
    bt = rb[idx]
    bt = np.transpose(bt, (1, 0, 3, 2)).reshape(128, 3 * C.H * 256)
    k = np.arange(128)[:, None]
    q = np.arange(128)[None, :]
    mask = np.concatenate([(q >= k), (k >= q)], axis=1).astype(np.float32)
    ident = np.eye(128, dtype=np.float32)
    shared = {
        "w_in": f(w_in[0]), "w_proj_a": f(w_proj_a[0]), "w_proj_b": f(w_proj_b[0]), "w_out": f(w_out[0]),
        "w_ff1": f(w_ff1[0]), "w_ff2": f(w_ff2[0]), "w_sp": f(w_spatial[0]), "vecs": f(vecs), "lnv": f(lnv),
        "bsp": f(bsp), "btab": f(bt), "mask": mask, "ident": ident,
    }
    xs = f(x)
    return [dict(shared, x=np.ascontiguousarray(xs[b])) for b in range(xs.shape[0])]


_NC_CACHE = {}


def kernel(**inputs):
    cfg = Cfg()
    in_maps = prep_inputs(cfg, **inputs)
    if "nc" not in _NC_CACHE:
        _NC_CACHE["nc"] = build_nc(cfg)
    nc = _NC_CACHE["nc"]
    res = run_bass_kernel_spmd(nc, in_maps, core_ids=list(range(len(in_maps))))
    out = np.stack([np.asarray(r["out"], dtype=np.float32) for r in res.results], axis=0)
    return out
```
